# Optimizing a Trainium2 kernel written in Bass

```python
import math
import jax, jax.numpy as jnp
from jax import lax
import numpy as np

D_MODEL = 1024
BATCH = 8
SEQ = 2048
DEPTH = 2

CHUNK = 64
Q_BLOCK = 128
N_MIXERS = 2
N_HEADS = 8
QK_DIM = 64
V_DIM = 2 * QK_DIM
QKV_COLS = N_HEADS * (4 * QK_DIM + V_DIM)
CONV_WIDTH = 3
D_FF = 2816
NORM_EPS = 1e-6
SUBLN_EPS = 1e-5
N_ATTN_LAYERS = (DEPTH + 1) // 2
N_CONV_LAYERS = DEPTH // 2

kernel_name = "chunk_causal_diffattn_shortconv_hybrid"


def rms_norm(x, g, eps=NORM_EPS):
    xf = x.astype(jnp.float32)
    y = xf * lax.rsqrt(jnp.mean(xf * xf, axis=-1, keepdims=True) + eps)
    return (y * g.astype(jnp.float32)).astype(x.dtype)


def causal_dwconv(x, w):
    k = w.shape[0]
    s = x.shape[1]
    xp = jnp.pad(x, ((0, 0), (k - 1, 0), (0, 0)))
    out = xp[:, 0:s, :] * w[0]
    for j in range(1, k):
        out = out + xp[:, j:j + s, :] * w[j]
    return out


def alibi_slopes(n_heads):
    return jnp.exp2(-8.0 * jnp.arange(1, n_heads + 1, dtype=jnp.float32) / n_heads)


def diff_attention(h, w_qkv, w_o, lq1, lk1, lq2, lk2, subln_g, lambda_init):
    b, s, _ = h.shape
    proj = h @ w_qkv
    qk_w = N_HEADS * QK_DIM
    q1, q2, k1, k2, v = jnp.split(proj, [qk_w, 2 * qk_w, 3 * qk_w, 4 * qk_w], axis=-1)
    scale = QK_DIM ** -0.5
    q1 = q1.reshape(b, s, N_HEADS, QK_DIM) * scale
    q2 = q2.reshape(b, s, N_HEADS, QK_DIM) * scale
    k1 = k1.reshape(b, s, N_HEADS, QK_DIM)
    k2 = k2.reshape(b, s, N_HEADS, QK_DIM)
    v = v.reshape(b, s, N_HEADS, V_DIM)
    lam = (jnp.exp(jnp.sum(lq1.astype(jnp.float32) * lk1.astype(jnp.float32)))
           - jnp.exp(jnp.sum(lq2.astype(jnp.float32) * lk2.astype(jnp.float32)))
           + lambda_init)
    slopes = alibi_slopes(N_HEADS)
    pos = jnp.arange(s, dtype=jnp.int32)
    outs = []
    for qb in range(s // Q_BLOCK):
        q0 = qb * Q_BLOCK
        kend = q0 + Q_BLOCK
        qp = pos[q0:kend]
        kp = pos[:kend]
        allowed = (kp[None, :] // CHUNK) <= (qp[:, None] // CHUNK)
        dist = jnp.abs(qp[:, None] - kp[None, :]).astype(jnp.float32)
        bias = jnp.where(allowed[None], -slopes[:, None, None] * dist[None], -jnp.inf)
        s1 = jnp.einsum('bqhd,bkhd->bhqk', q1[:, q0:kend], k1[:, :kend]).astype(jnp.float32) + bias
        s2 = jnp.einsum('bqhd,bkhd->bhqk', q2[:, q0:kend], k2[:, :kend]).astype(jnp.float32) + bias
        a = jax.nn.softmax(s1, axis=-1) - lam * jax.nn.softmax(s2, axis=-1)
        outs.append(jnp.einsum('bhqk,bkhe->bqhe', a.astype(v.dtype), v[:, :kend]))
    o = jnp.concatenate(outs, axis=1)
    o = rms_norm(o, subln_g, SUBLN_EPS) * (1.0 - lambda_init)
    return o.reshape(b, s, N_HEADS * V_DIM) @ w_o


def short_conv_mixer(h, w_in, conv_w, w_out):
    b_gate, c_gate, hv = jnp.split(h @ w_in, 3, axis=-1)
    return (b_gate * causal_dwconv(c_gate * hv, conv_w)) @ w_out


def conv_glu_ffn(h, w_up, conv_w, w_down):
    u = causal_dwconv(h @ w_up, conv_w)
    gate, val = jnp.split(u, 2, axis=-1)
    return (jax.nn.silu(gate) * val) @ w_down


def setup_inputs(seed: int = 0) -> dict:
    key = jax.random.key(seed)
    ks = jax.random.split(key, 16)
    f32 = jnp.float32
    d = D_MODEL

    def nrm(k, shape, scale):
        return jax.random.normal(k, shape, f32) * scale

    return {
        "x": jax.random.normal(ks[0], (BATCH, SEQ, d), f32),
        "norm_g": 1.0 + nrm(ks[1], (DEPTH, 4, d), 0.05),
        "attn_w_qkv": nrm(ks[2], (N_ATTN_LAYERS, d, QKV_COLS), d ** -0.5),
        "attn_w_o": nrm(ks[3], (N_ATTN_LAYERS, N_HEADS * V_DIM, d), (N_HEADS * V_DIM) ** -0.5),
        "attn_lambda_q1": nrm(ks[4], (N_ATTN_LAYERS, QK_DIM), 0.1),
        "attn_lambda_k1": nrm(ks[5], (N_ATTN_LAYERS, QK_DIM), 0.1),
        "attn_lambda_q2": nrm(ks[6], (N_ATTN_LAYERS, QK_DIM), 0.1),
        "attn_lambda_k2": nrm(ks[7], (N_ATTN_LAYERS, QK_DIM), 0.1),
        "attn_subln_g": 1.0 + nrm(ks[8], (N_ATTN_LAYERS, V_DIM), 0.05),
        "conv_w_in": nrm(ks[9], (N_CONV_LAYERS, d, 3 * d), d ** -0.5),
        "conv_w": nrm(ks[10], (N_CONV_LAYERS, CONV_WIDTH, d), CONV_WIDTH ** -0.5),
        "conv_w_out": nrm(ks[11], (N_CONV_LAYERS, d, d), d ** -0.5),
        "ffn_w_up": nrm(ks[12], (DEPTH, d, 2 * D_FF), d ** -0.5),
        "ffn_conv_w": nrm(ks[13], (DEPTH, CONV_WIDTH, 2 * D_FF), CONV_WIDTH ** -0.5),
        "ffn_w_down": nrm(ks[14], (DEPTH, D_FF, d), D_FF ** -0.5),
    }


def reference(x, norm_g, attn_w_qkv, attn_w_o, attn_lambda_q1, attn_lambda_k1,
              attn_lambda_q2, attn_lambda_k2, attn_subln_g, conv_w_in, conv_w,
              conv_w_out, ffn_w_up, ffn_conv_w, ffn_w_down):
    for layer in range(DEPTH):
        g = norm_g[layer]
        h = rms_norm(x, g[0])
        i = layer // N_MIXERS
        if layer % N_MIXERS == 0:
            lambda_init = 0.8 - 0.6 * math.exp(-0.3 * layer)
            m = diff_attention(h, attn_w_qkv[i], attn_w_o[i], attn_lambda_q1[i],
                               attn_lambda_k1[i], attn_lambda_q2[i], attn_lambda_k2[i],
                               attn_subln_g[i], lambda_init)
        else:
            m = short_conv_mixer(h, conv_w_in[i], conv_w[i], conv_w_out[i])
        x = x + rms_norm(m, g[1])
        h = rms_norm(x, g[2])
        x = x + rms_norm(conv_glu_ffn(h, ffn_w_up[layer], ffn_conv_w[layer], ffn_w_down[layer]), g[3])
    return x
```

```python
import math
from contextlib import ExitStack

import numpy as np
import concourse.bass as bass
import concourse.mybir as mybir
from concourse.bass_utils import run_bass_kernel_spmd

F32 = mybir.dt.float32
BF16 = mybir.dt.bfloat16
AF = mybir.ActivationFunctionType
ALU = mybir.AluOpType

S_LEN = 2048
D = 1024
NB = 4
KC = 8
DFF = 2816
NFI = 22
H = 8
NORM_EPS = 1e-6
SUBLN_EPS = 1e-5
NEG_BIG = -60000.0

ENGS = ("pe", "act", "dve", "pool", "sp")


def OP(name, *args, **kw):
    return lambda e: getattr(e, name)(*args, **kw)


class Buf:
    __slots__ = ("name", "w", "r")

    def __init__(self, name=""):
        self.name = name
        self.w = None
        self.r = {}


class Sched:
    def __init__(self):
        self.ops = {e: [] for e in ENGS}
        self.cnt = {}
        self.seen = {e: {} for e in ENGS}
        self.dma_keys = []

    def _waits(self, eng, reads, writes):
        deps = {}

        def add(tok):
            if tok is None:
                return
            k, v = tok
            if deps.get(k, 0) < v:
                deps[k] = v

        for b in reads:
            add(b.w)
        for b in writes:
            add(b.w)
            for k, v in b.r.items():
                add((k, v))
        waits = []
        for k, v in deps.items():
            if eng == "pe" and k == "pe":
                continue
            if self.seen[eng].get(k, 0) >= v:
                continue
            self.seen[eng][k] = v
            waits.append((k, v))
        return waits

    def _commit(self, tok, reads, writes):
        for b in reads:
            if b.r.get(tok[0], 0) < tok[1]:
                b.r[tok[0]] = tok[1]
        for b in writes:
            b.w = tok
            b.r = {}

    def emit(self, eng, fns, reads=(), writes=()):
        if not isinstance(fns, (list, tuple)):
            fns = [fns]
        waits = self._waits(eng, reads, writes)
        self.cnt[eng] = self.cnt.get(eng, 0) + 1
        tok = (eng, self.cnt[eng])
        self.ops[eng].append((waits, list(fns), tok, 1))
        self._commit(tok, reads, writes)
        return tok

    def dma(self, qeng, fn, key, reads=(), writes=(), nowait=False):
        waits = [] if nowait else self._waits(qeng, reads, writes)
        if key not in self.cnt:
            self.cnt[key] = 0
            self.dma_keys.append(key)
        self.cnt[key] += 16
        tok = (key, self.cnt[key])
        self.ops[qeng].append((waits, [fn], tok, 16))
        self._commit(tok, reads, writes)
        return tok

    def barrier(self):
        for e in ENGS:
            waits = []
            for k in ("pe", "act", "dve", "pool"):
                v = self.cnt.get(k, 0)
                if v == 0 or (e == k and e == "pe"):
                    continue
                if self.seen[e].get(k, 0) >= v:
                    continue
                self.seen[e][k] = v
                waits.append((k, v))
            if waits:
                self.ops[e].append((waits, [], None, 0))

    def wait_all(self, eng, toks):
        waits = []
        for k, v in toks:
            if self.seen[eng].get(k, 0) >= v:
                continue
            self.seen[eng][k] = v
            waits.append((k, v))
        if waits:
            self.ops[eng].append((waits, [], None, 0))


def build_program(layers=(0, 1), parts=('mix', 'ffn')):
    nc = bass.Bass("TRN2", target_bir_lowering=False)
    S = Sched()

    def din(name, shape):
        return nc.dram_tensor(name, list(shape), F32, kind="ExternalInput").ap()

    x_d = din("x", (S_LEN, D))
    norm_g_d = din("norm_g", (2, 4, D))
    wqkv_d = din("attn_w_qkv", (1, D, 3072))
    wo_d = din("attn_w_o", (1, D, D))
    lq1_d = din("attn_lambda_q1", (1, 64))
    lk1_d = din("attn_lambda_k1", (1, 64))
    lq2_d = din("attn_lambda_q2", (1, 64))
    lk2_d = din("attn_lambda_k2", (1, 64))
    subg_d = din("attn_subln_g", (1, 128))
    win_d = din("conv_w_in", (1, D, 3072))
    cw_d = din("conv_w", (1, 3, D))
    wout_d = din("conv_w_out", (1, D, D))
    wup_d = din("ffn_w_up", (2, D, 2 * DFF))
    fcw_d = din("ffn_conv_w", (2, 3, 2 * DFF))
    wdn_d = din("ffn_w_down", (2, DFF, D))
    ident_d = din("c_ident", (128, 128))
    ones_d = din("c_ones", (128, 128))
    kaug_d = din("c_kaug", (3, 128))
    qaug_d = din("c_qaug", (3, 512))
    maskd_d = din("c_maskd", (128, 128))
    btab_d = din("c_btab", (128, 128))
    y_d = nc.dram_tensor("y", [S_LEN, D], F32, kind="ExternalOutput").ap()

    es = ExitStack()
    with es:
        E = es.enter_context
        xT = E(nc.sbuf_tensor("xT", [128, KC, S_LEN], F32))
        hT = E(nc.sbuf_tensor("hT", [128, KC, S_LEN], BF16))
        NSLOT = 3
        wring = E(nc.sbuf_tensor("wring", [128, NSLOT, 4096], BF16))
        identf = E(nc.sbuf_tensor("identf", [128, 128], F32))
        identb = E(nc.sbuf_tensor("identb", [128, 128], BF16))
        onesb = E(nc.sbuf_tensor("onesb", [128, 128], BF16))
        kaug = E(nc.sbuf_tensor("kaug", [3, 128], BF16))
        qaug = E(nc.sbuf_tensor("qaug", [3, 512], BF16))
        maskd = E(nc.sbuf_tensor("maskd", [128, 128], BF16))
        btab = E(nc.sbuf_tensor("btab", [128, 128], F32))
        mhalf = E(nc.sbuf_tensor("mhalf", [128, 512], F32))
        g32 = E(nc.sbuf_tensor("g32", [128, 64], F32))
        fcw = E(nc.sbuf_tensor("fcw", [128, 264], F32))
        cwt = E(nc.sbuf_tensor("cwt", [128, 24], F32))
        gsub = E(nc.sbuf_tensor("gsub", [128, 128], F32))
        lamt = E(nc.sbuf_tensor("lamt", [128, 8], F32))
        pstage = E(nc.sbuf_tensor("pstage", [128, 128], F32))
        lstage = E(nc.sbuf_tensor("lstage", [128, 4, 64], F32))
        ARENA_W = 20300
        arena = E(nc.sbuf_tensor("arena", [128, ARENA_W], F32))
        ps = E(nc.psum_tensor("ps", [128, 8, 512], F32))
        pb = [Buf("pb%d" % i) for i in range(8)]

        class Arena:
            def __init__(self):
                self.off = 0

            def reset(self):
                self.off = 0

            def f32(self, n):
                a = arena[:, self.off:self.off + n]
                self.off += n
                assert self.off <= ARENA_W, self.off
                return a

            def bf16(self, n):
                w = (n + 1) // 2
                a = arena[:, self.off:self.off + w].bitcast(BF16)
                self.off += w
                assert self.off <= ARENA_W, self.off
                return a

        AR = Arena()

        xTb = [[Buf("xT%d_%d" % (c, tb)) for tb in range(NB)] for c in range(KC)]
        hTb = [[Buf("hT%d_%d" % (c, tb)) for tb in range(NB)] for c in range(KC)]
        cbuf = Buf("consts")

        bank_rr = [0]

        def next_bank(lo=0, n=4):
            b = lo + bank_rr[0] % n
            bank_rr[0] += 1
            return b

        wslot_buf = [Buf("wslot%d" % i) for i in range(NSLOT)]
        wplan = []
        wstate = {"issued": 0, "taken": 0}

        def w_issue_upto(n):
            while wstate["issued"] < min(n, len(wplan)):
                i = wstate["issued"]
                slot = i % NSLOT
                for pi, (dst_fn, src) in enumerate(wplan[i]):
                    dst = dst_fn(slot)
                    S.dma("pool", OP('dma_start', out=dst, in_=src),
                          ("w", slot), writes=[wslot_buf[slot]], nowait=(pi > 0))
                wstate["issued"] += 1

        def w_next():
            i = wstate["taken"]
            w_issue_upto(i + 1)
            wstate["taken"] += 1
            return i % NSLOT, wslot_buf[i % NSLOT]

        def w_prefetch():
            w_issue_upto(wstate["taken"] + NSLOT)

        def wview(slot, dims):
            n = int(np.prod(dims))
            a = wring[:, slot, 0:n]
            if len(dims) == 2:
                return a.rearrange("p (a b) -> p a b", a=dims[0])
            if len(dims) == 3:
                return a.rearrange("p (a b c) -> p a b c", a=dims[0], b=dims[1])
            return a

        def plan_weights():
            def rows(wd):
                return wd.rearrange("(kc p) n -> p kc n", p=128)

            for L in layers:
                if 'mix' not in parts:
                    pass
                elif L == 0:
                    w = rows(wqkv_d[0])
                    for g in range(2):
                        wplan.append([(lambda s: wview(s, (KC, 512)), w[:, :, 2048 + 512 * g:2048 + 512 * (g + 1)])])
                    for h in range(H):
                        ld = []
                        for j, base in enumerate((0, 512, 1024, 1536)):
                            ld.append((lambda s, j=j: wview(s, (KC, 256))[:, :, 64 * j:64 * (j + 1)],
                                       w[:, :, base + 64 * h:base + 64 * (h + 1)]))
                        wplan.append(ld)
                    w = rows(wo_d[0])
                    for g in range(2):
                        wplan.append([(lambda s: wview(s, (KC, 512)), w[:, :, 512 * g:512 * (g + 1)])])
                else:
                    w = rows(win_d[0])
                    for j in range(KC):
                        ld = []
                        for sec in range(3):
                            ld.append((lambda s, sec=sec: wview(s, (KC, 3, 128))[:, :, sec, :],
                                       w[:, :, sec * 1024 + 128 * j:sec * 1024 + 128 * (j + 1)]))
                        wplan.append(ld)
                    w = rows(wout_d[0])
                    for g in range(2):
                        wplan.append([(lambda s: wview(s, (KC, 512)), w[:, :, 512 * g:512 * (g + 1)])])
                wu = rows(wup_d[L])
                wd = wdn_d[L].rearrange("(fi p) n -> p fi n", p=128)
                for tb in range(NB if 'ffn' in parts else 0):
                    for fp in range(NFI // 2):
                        ld = []
                        for gv in range(2):
                            ld.append((lambda s, gv=gv: wview(s, (2, KC, 256))[:, gv, :, :],
                                       wu[:, :, gv * DFF + 256 * fp:gv * DFF + 256 * (fp + 1)]))
                        wplan.append(ld)
                    for c in range(KC):
                        wplan.append([(lambda s: wview(s, (NFI, 128)), wd[:, :, 128 * c:128 * (c + 1)])])

        plan_weights()

        def setup():
            def cload(q, dst, src, key):
                S.dma(q, OP('dma_start', out=dst, in_=src), key, writes=[cbuf])

            cload("sp", identf[:, :], ident_d[:, :], "c0")
            cload("sp", btab[:, :], btab_d[:, :], "c0")
            cload("pool", identb[:, :], ident_d[:, :], "c1")
            cload("pool", onesb[:, :], ones_d[:, :], "c1")
            cload("pool", kaug[:, :], kaug_d[:, :], "c1")
            cload("pool", qaug[:, :], qaug_d[:, :], "c1")
            cload("pool", maskd[:, :], maskd_d[:, :], "c1")
            S.emit("pool", OP('memset', mhalf[:, :], -0.5), writes=[cbuf])
            pst = Buf("pstage")

            def tr_param(src2d, nrows, dst, scale):
                S.dma("sp", (OP('dma_start', out=pstage[0:nrows, :], in_=src2d)), "c2", writes=[pst])
                bank = next_bank()
                S.emit("pe", OP('transpose', ps[:, bank, 0:nrows], pstage[0:nrows, :], identf[0:nrows, 0:nrows]),
                       reads=[pst, cbuf], writes=[pb[bank]])
                S.emit("dve", OP('tensor_scalar', out=dst, in0=ps[:, bank, 0:nrows], scalar1=float(scale),
                                                       scalar2=None, op0=ALU.mult),
                       reads=[pb[bank]], writes=[cbuf])

            tr_param(norm_g_d.rearrange("l i (c p) -> (l i c) p", p=128), 64, g32[:, :], 32.0)
            f2 = fcw_d.rearrange("l j (c p) -> (l j c) p", p=128)
            for i in range(3):
                tr_param(f2[88 * i:88 * (i + 1), :], 88, fcw[:, 88 * i:88 * (i + 1)], 1.0)
            tr_param(cw_d.rearrange("l j (c p) -> (l j c) p", p=128), 24, cwt[:, :], 1.0)
            lam_init = 0.8 - 0.6 * math.exp(-0.3 * 0)
            S.dma("sp", (OP('dma_start', out=gsub[:, :], in_=subg_d[0:1, :].broadcast_to([128, 128]))),
                  "c3", writes=[cbuf])
            S.emit("dve", OP('tensor_scalar', out=gsub[:, :], in0=gsub[:, :],
                                                   scalar1=float((1.0 - lam_init) * math.sqrt(128.0)),
                                                   scalar2=None, op0=ALU.mult), reads=[cbuf], writes=[cbuf])
            for i, ldd in enumerate((lq1_d, lk1_d, lq2_d, lk2_d)):
                S.dma("sp", (OP('dma_start', out=lstage[:, i, :],
                                                                 in_=ldd[0:1, :].broadcast_to([128, 64]))),
                      "c3", writes=[cbuf])
            S.emit("dve", OP('tensor_tensor', out=lstage[:, 0, :], in0=lstage[:, 0, :], in1=lstage[:, 1, :],
                                                   op=ALU.mult), reads=[cbuf], writes=[cbuf])
            S.emit("dve", OP('tensor_tensor', out=lstage[:, 2, :], in0=lstage[:, 2, :], in1=lstage[:, 3, :],
                                                   op=ALU.mult), reads=[cbuf], writes=[cbuf])
            S.emit("dve", OP('reduce_sum', out=lamt[:, 0:1], in_=lstage[:, 0, :], axis=mybir.AxisListType.X),
                   reads=[cbuf], writes=[cbuf])
            S.emit("dve", OP('reduce_sum', out=lamt[:, 1:2], in_=lstage[:, 2, :], axis=mybir.AxisListType.X),
                   reads=[cbuf], writes=[cbuf])
            S.emit("act", OP('activation', out=lamt[:, 2:4], in_=lamt[:, 0:2], func=AF.Exp),
                   reads=[cbuf], writes=[cbuf])
            S.emit("dve", OP('tensor_tensor', out=lamt[:, 4:5], in0=lamt[:, 2:3], in1=lamt[:, 3:4],
                                                   op=ALU.subtract), reads=[cbuf], writes=[cbuf])
            S.emit("dve", OP('tensor_scalar', out=lamt[:, 5:6], in0=lamt[:, 4:5], scalar1=float(lam_init),
                                                   scalar2=None, op0=ALU.add), reads=[cbuf], writes=[cbuf])

        def phase_load_x():
            AR.reset()
            xs = AR.f32(8 * 1024).rearrange("p (s n) -> p s n", s=8)
            xsb = [Buf("xs%d" % i) for i in range(8)]

            def load(t):
                sl = t % 8
                S.dma("sp", (OP('dma_start', out=xs[:, sl, :], in_=x_d[128 * t:128 * (t + 1), :])),
                      ("xs", sl), writes=[xsb[sl]])

            for t in range(8):
                load(t)
            for tb in range(NB):
                for c in range(KC):
                    bank = next_bank()
                    fns = []
                    for j in range(4):
                        sl = (4 * tb + j) % 8
                        fns.append(OP('transpose',
                            ps[:, bank, 128 * j:128 * (j + 1)], xs[:, sl, 128 * c:128 * (c + 1)], identf[:, :]))
                    S.emit("pe", fns, reads=[xsb[(4 * tb + j) % 8] for j in range(4)] + [cbuf], writes=[pb[bank]])
                    dst = xT[:, c, 512 * tb:512 * (tb + 1)]
                    if c % 2 == 0:
                        S.emit("act", OP('activation', out=dst, in_=ps[:, bank, :], func=AF.Copy),
                               reads=[pb[bank]], writes=[xTb[c][tb]])
                    else:
                        S.emit("dve", OP('tensor_copy', out=dst, in_=ps[:, bank, :]),
                               reads=[pb[bank]], writes=[xTb[c][tb]])
                if tb + 2 < NB:
                    for j in range(4):
                        load(4 * (tb + 2) + j)

        out_toks = []

        def phase_store_y():
            AR.reset()
            ys = AR.f32(4 * 1024).rearrange("p (s n) -> p s n", s=4)
            ysb = [Buf("ys%d" % i) for i in range(4)]
            for t in range(16):
                tb = t // 4
                sl = t % 4
                for half in range(2):
                    bank = next_bank()
                    fns = []
                    for j in range(4):
                        c = 4 * half + j
                        fns.append(OP('transpose',
                            ps[:, bank, 128 * j:128 * (j + 1)], xT[:, c, 128 * t:128 * (t + 1)], identf[:, :]))
                    S.emit("pe", fns, reads=[xTb[4 * half + j][tb] for j in range(4)] + [cbuf], writes=[pb[bank]])
                    dst = ys[:, sl, 512 * half:512 * (half + 1)]
                    if half == 0:
                        S.emit("act", OP('activation', out=dst, in_=ps[:, bank, :], func=AF.Copy),
                               reads=[pb[bank]], writes=[ysb[sl]])
                    else:
                        S.emit("dve", OP('tensor_copy', out=dst, in_=ps[:, bank, :]),
                               reads=[pb[bank]], writes=[ysb[sl]])
                tok = S.dma("sp", (OP('dma_start', out=y_d[128 * t:128 * (t + 1), :], in_=ys[:, sl, :])),
                            ("ys", sl), reads=[ysb[sl]])
                out_toks.append(tok)

        class NormTmp:
            def __init__(self):
                self.sq = AR.bf16(4 * 512).rearrange("p (s n) -> p s n", s=4)
                self.sqb = [Buf("sq%d" % i) for i in range(4)]
                self.t1 = AR.f32(2 * 512).rearrange("p (s n) -> p s n", s=2)
                self.t1b = [Buf("t1_%d" % i) for i in range(2)]
                self.rstd = AR.f32(2 * 512).rearrange("p (s n) -> p s n", s=2)
                self.rstdb = [Buf("rstd%d" % i) for i in range(2)]
                self.k = 0
                self.r = 0

            def next_sq(self):
                i = self.k % 4
                self.k += 1
                return i

            def next_r(self):
                i = self.r % 2
                self.r += 1
                return i

        def emit_rstd(nt, ssbank, eps_total):
            r = nt.next_r()
            S.emit("dve", OP('tensor_scalar', out=nt.t1[:, r, :], in0=ps[:, ssbank, :], scalar1=float(eps_total),
                                                   scalar2=None, op0=ALU.add),
                   reads=[pb[ssbank]], writes=[nt.t1b[r]])
            S.emit("pool", OP('tensor_tensor', out=nt.rstd[:, r, :], in0=nt.t1[:, r, :], in1=mhalf[:, :], op=ALU.pow),
                   reads=[nt.t1b[r], cbuf], writes=[nt.rstdb[r]])
            return r

        def prenorm(nt, gidx, tbs):
            for tb in tbs:
                blk = slice(512 * tb, 512 * (tb + 1))
                ssbank = 7
                for c in range(KC):
                    i = nt.next_sq()
                    S.emit("act", OP('activation', out=nt.sq[:, i, :], in_=xT[:, c, blk], func=AF.Square),
                           reads=[xTb[c][tb]], writes=[nt.sqb[i]])
                    S.emit("pe", OP('matmul', ps[:, ssbank, :], onesb[:, :], nt.sq[:, i, :],
                                                             start=(c == 0), stop=(c == KC - 1)),
                           reads=[nt.sqb[i], cbuf], writes=[pb[ssbank]])
                r = emit_rstd(nt, ssbank, D * NORM_EPS)
                for c in range(KC):
                    S.emit("dve", OP('scalar_tensor_tensor',
                        out=hT[:, c, blk], in0=xT[:, c, blk], scalar=g32[:, gidx * 8 + c:gidx * 8 + c + 1],
                        in1=nt.rstd[:, r, :], op0=ALU.mult, op1=ALU.mult),
                           reads=[xTb[c][tb], nt.rstdb[r], cbuf], writes=[hTb[c][tb]])

        class PostTmp:
            def __init__(self):
                self.msb = AR.f32(KC * 512).rearrange("p (c n) -> p c n", c=KC)
                self.msbb = [Buf("msb%d" % i) for i in range(KC)]
                self.tp = AR.f32(2 * 512).rearrange("p (s n) -> p s n", s=2)
                self.tpb = [Buf("tp%d" % i) for i in range(2)]
                self.k = 0

        def post_block(nt, pt, gidx, tb, mm_group):
            blk = slice(512 * tb, 512 * (tb + 1))
            ssbank = 7
            pending = None

            def ss_mm(c, i):
                S.emit("pe", OP('matmul', ps[:, ssbank, :], onesb[:, :], nt.sq[:, i, :],
                                                start=(c == 0), stop=(c == KC - 1)),
                       reads=[nt.sqb[i], cbuf], writes=[pb[ssbank]])

            for c in range(KC):
                bank = next_bank()
                mm_group(c, bank)
                if pending is not None:
                    ss_mm(*pending)
                S.emit("act", OP('activation', out=pt.msb[:, c, :], in_=ps[:, bank, :], func=AF.Copy),
                       reads=[pb[bank]], writes=[pt.msbb[c]])
                i = nt.next_sq()
                S.emit("act", OP('activation', out=nt.sq[:, i, :], in_=pt.msb[:, c, :], func=AF.Square),
                       reads=[pt.msbb[c]], writes=[nt.sqb[i]])
                pending = (c, i)
            ss_mm(*pending)
            r = emit_rstd(nt, ssbank, D * NORM_EPS)
            for c in range(KC):
                j = pt.k % 2
                pt.k += 1
                S.emit("dve", OP('scalar_tensor_tensor',
                    out=pt.tp[:, j, :], in0=pt.msb[:, c, :], scalar=g32[:, gidx * 8 + c:gidx * 8 + c + 1],
                    in1=nt.rstd[:, r, :], op0=ALU.mult, op1=ALU.mult),
                       reads=[pt.msbb[c], nt.rstdb[r], cbuf], writes=[pt.tpb[j]])
                S.emit("pool", OP('tensor_tensor', out=xT[:, c, blk], in0=xT[:, c, blk], in1=pt.tp[:, j, :],
                                                                  op=ALU.add),
                       reads=[pt.tpb[j], xTb[c][tb]], writes=[xTb[c][tb]])

        def proj_post(nt, pt, gidx, in_ap, in_bufs):
            s0, b0 = w_next()
            s1, b1 = w_next()
            wt = [wview(s0, (KC, 512)), wview(s1, (KC, 512))]
            wb = [b0, b1]
            for tb in range(NB):
                def grp(c, bank, tb=tb):
                    w = wt[c // 4]
                    col = 128 * (c % 4)
                    fns = [(OP('matmul', ps[:, bank, :], w[:, kc, col:col + 128], in_ap(kc, tb),
                                                      start=(kc == 0), stop=(kc == KC - 1))) for kc in range(KC)]
                    S.emit("pe", fns, reads=[wb[c // 4]] + in_bufs(tb), writes=[pb[bank]])

                post_block(nt, pt, gidx, tb, grp)
            w_prefetch()

        def phase_ffn(L):
            AR.reset()
            nt = NormTmp()
            pt = PostTmp()
            aT = AR.bf16(NFI * 512).rearrange("p (f n) -> p f n", f=NFI)
            aTb = [Buf("aT%d" % i) for i in range(NFI)]
            NU = 4
            ug = AR.f32(NU * 516).rearrange("p (s n) -> p s n", s=NU)
            ugb = [Buf("ug%d" % i) for i in range(NU)]
            tcv = AR.f32(NU * 512).rearrange("p (s n) -> p s n", s=NU)
            tcb = [Buf("tc%d" % i) for i in range(NU)]
            sg = AR.bf16(2 * 512).rearrange("p (s n) -> p s n", s=2)
            sgb = [Buf("sg%d" % i) for i in range(2)]
            halo = AR.f32(44 * 2).rearrange("p (c n) -> p c n", c=44)
            halob = [Buf("halo%d" % i) for i in range(44)]
            uk = [0]
            sk = [0]
            for tb in range(NB):
                blk = slice(512 * tb, 512 * (tb + 1))
                prenorm(nt, L * 4 + 2, [tb])
                for fp in range(NFI // 2):
                    slot, wbuf = w_next()
                    wt = wview(slot, (2, KC, 256))
                    for f2 in range(2):
                        fi = 2 * fp + f2
                        tcs = []
                        for gv in range(2):
                            ch = gv * NFI + fi
                            bank = next_bank()
                            fns = [(OP('matmul',
                                ps[:, bank, :], wt[:, gv, kc, 128 * f2:128 * (f2 + 1)], hT[:, kc, blk],
                                start=(kc == 0), stop=(kc == KC - 1))) for kc in range(KC)]
                            S.emit("pe", fns, reads=[wbuf] + [hTb[kc][tb] for kc in range(KC)], writes=[pb[bank]])
                            u = uk[0] % NU
                            uk[0] += 1
                            if tb == 0:
                                S.emit("pool", OP('memset', ug[:, u, 0:2], 0.0), writes=[ugb[u]])
                            else:
                                S.emit("pool", OP('tensor_copy', out=ug[:, u, 0:2], in_=halo[:, ch, :]),
                                       reads=[halob[ch]], writes=[ugb[u]])
                            S.emit("act", OP('activation', out=ug[:, u, 2:514], in_=ps[:, bank, :],
                                                                              func=AF.Copy),
                                   reads=[pb[bank]], writes=[ugb[u]])
                            if tb < NB - 1:
                                S.emit("pool", OP('tensor_copy', out=halo[:, ch, :], in_=ug[:, u, 512:514]),
                                       reads=[ugb[u]], writes=[halob[ch]])
                            wi = lambda j, ch=ch: fcw[:, (L * 3 + j) * 44 + ch:(L * 3 + j) * 44 + ch + 1]
                            S.emit("dve", OP('tensor_scalar', out=tcv[:, u, :], in0=ug[:, u, 2:514],
                                                                             scalar1=wi(2), scalar2=None, op0=ALU.mult),
                                   reads=[ugb[u], cbuf], writes=[tcb[u]])
                            S.emit("dve", OP('scalar_tensor_tensor',
                                out=tcv[:, u, :], in0=ug[:, u, 1:513], scalar=wi(1), in1=tcv[:, u, :],
                                op0=ALU.mult, op1=ALU.add), reads=[ugb[u], tcb[u], cbuf], writes=[tcb[u]])
                            S.emit("dve", OP('scalar_tensor_tensor',
                                out=tcv[:, u, :], in0=ug[:, u, 0:512], scalar=wi(0), in1=tcv[:, u, :],
                                op0=ALU.mult, op1=ALU.add), reads=[ugb[u], tcb[u], cbuf], writes=[tcb[u]])
                            tcs.append(u)
                        s = sk[0] % 2
                        sk[0] += 1
                        S.emit("act", OP('activation', out=sg[:, s, :], in_=tcv[:, tcs[0], :], func=AF.Silu),
                               reads=[tcb[tcs[0]]], writes=[sgb[s]])
                        S.emit("dve", OP('tensor_tensor', out=aT[:, fi, :], in0=sg[:, s, :], in1=tcv[:, tcs[1], :], op=ALU.mult),
                               reads=[sgb[s], tcb[tcs[1]]], writes=[aTb[fi]])
                    w_prefetch()
                wds = {}

                def grp(c, bank):
                    slot, wbuf = w_next()
                    wd = wview(slot, (NFI, 128))
                    fns = [(OP('matmul', ps[:, bank, :], wd[:, fi, :], aT[:, fi, :],
                                                      start=(fi == 0), stop=(fi == NFI - 1))) for fi in range(NFI)]
                    S.emit("pe", fns, reads=[wbuf] + aTb, writes=[pb[bank]])
                    w_prefetch()

                post_block(nt, pt, L * 4 + 3, tb, grp)

        def phase_conv(L):
            AR.reset()
            nt = NormTmp()
            pt = PostTmp()
            gT = AR.bf16(KC * S_LEN).rearrange("p (c n) -> p c n", c=KC)
            gTb = [[Buf("gT%d_%d" % (c, tb)) for tb in range(NB)] for c in range(KC)]
            uc = AR.f32(2 * 512).rearrange("p (s n) -> p s n", s=2)
            ucb = [Buf("uc%d" % i) for i in range(2)]
            z = AR.f32(2 * 516).rearrange("p (s n) -> p s n", s=2)
            zb = [Buf("z%d" % i) for i in range(2)]
            tcv = AR.f32(2 * 512).rearrange("p (s n) -> p s n", s=2)
            tcb = [Buf("tcc%d" % i) for i in range(2)]
            prenorm(nt, L * 4 + 0, range(NB))
            k = 0
            for j in range(KC):
                slot, wbuf = w_next()
                wt = wview(slot, (KC, 3, 128))
                for tb in range(NB):
                    blk = slice(512 * tb, 512 * (tb + 1))
                    banks = []
                    for sec in range(3):
                        bank = next_bank(0, 6)
                        fns = [(OP('matmul',
                            ps[:, bank, :], wt[:, kc, sec, :], hT[:, kc, blk],
                            start=(kc == 0), stop=(kc == KC - 1))) for kc in range(KC)]
                        S.emit("pe", fns, reads=[wbuf] + [hTb[kc][tb] for kc in range(KC)], writes=[pb[bank]])
                        banks.append(bank)
                    bb, bc, bh = banks
                    r = k % 2
                    k += 1
                    S.emit("act", OP('activation', out=uc[:, r, :], in_=ps[:, bc, :], func=AF.Copy),
                           reads=[pb[bc]], writes=[ucb[r]])
                    if tb == 0:
                        S.emit("pool", OP('memset', z[:, r, 0:2], 0.0), writes=[zb[r]])
                    else:
                        S.emit("pool", OP('tensor_copy', out=z[:, r, 0:2], in_=z[:, 1 - r, 512:514]),
                               reads=[zb[1 - r]], writes=[zb[r]])
                    S.emit("dve", OP('tensor_tensor', out=z[:, r, 2:514], in0=uc[:, r, :], in1=ps[:, bh, :],
                                                                      op=ALU.mult),
                           reads=[ucb[r], pb[bh]], writes=[zb[r]])
                    wi = lambda jj, j=j: cwt[:, jj * 8 + j:jj * 8 + j + 1]
                    S.emit("dve", OP('tensor_scalar', out=tcv[:, r, :], in0=z[:, r, 2:514], scalar1=wi(2),
                                                                     scalar2=None, op0=ALU.mult),
                           reads=[zb[r], cbuf], writes=[tcb[r]])
                    S.emit("dve", OP('scalar_tensor_tensor',
                        out=tcv[:, r, :], in0=z[:, r, 1:513], scalar=wi(1), in1=tcv[:, r, :], op0=ALU.mult, op1=ALU.add),
                           reads=[zb[r], tcb[r], cbuf], writes=[tcb[r]])
                    S.emit("dve", OP('scalar_tensor_tensor',
                        out=tcv[:, r, :], in0=z[:, r, 0:512], scalar=wi(0), in1=tcv[:, r, :], op0=ALU.mult, op1=ALU.add),
                           reads=[zb[r], tcb[r], cbuf], writes=[tcb[r]])
                    S.emit("dve", OP('tensor_tensor',
                        out=gT[:, j, blk], in0=tcv[:, r, :], in1=ps[:, bb, :], op=ALU.mult),
                           reads=[tcb[r], pb[bb]], writes=[gTb[j][tb]])
                w_prefetch()
            proj_post(nt, pt, L * 4 + 1,
                      lambda kc, tb: gT[:, kc, 512 * tb:512 * (tb + 1)],
                      lambda tb: [gTb[kc][tb] for kc in range(KC)])

        def phase_attn(L):
            AR.reset()
            vaug = AR.bf16(H * 16 * 130).rearrange("p (h t e) -> p h t e", h=H, t=16)
            vb = [[Buf("v%d_%d" % (g, t)) for t in range(16)] for g in range(2)]
            oT = AR.bf16(H * S_LEN).rearrange("p (h n) -> p h n", h=H)
            oTb = [[Buf("oT%d_%d" % (h, tb)) for tb in range(NB)] for h in range(H)]
            QQ = AR.bf16(S_LEN)
            KK = AR.bf16(S_LEN)
            QQb = [Buf("QQ%d" % tb) for tb in range(NB)]
            KKb = [Buf("KK%d" % tb) for tb in range(NB)]
            Pt = AR.bf16(2 * 2 * 512).rearrange("p (s m n) -> p s m n", s=2, m=2)
            Ptb = [Buf("Pt%d" % i) for i in range(2)]
            NEP = 2
            rl = AR.f32(NEP * 4).rearrange("p (s n) -> p s n", s=NEP)
            t2 = AR.f32(NEP * 128).rearrange("p (s n) -> p s n", s=NEP)
            ot = AR.f32(NEP * 128).rearrange("p (s n) -> p s n", s=NEP)
            junk = t2
            sst = AR.f32(NEP * 4).rearrange("p (s n) -> p s n", s=NEP)
            epb = [Buf("ep%d" % i) for i in range(NEP)]
            NON = 2
            on = AR.bf16(NON * 128).rearrange("p (s n) -> p s n", s=NON)
            onb = [Buf("on%d" % i) for i in range(NON)]
            AR.off = 8320
            nt0 = NormTmp()
            prenorm(nt0, L * 4 + 0, range(NB))
            S.barrier()
            S.emit("pool", OP('memset', vaug[:, :, :, 128:129], 1.0), writes=[b for g in vb for b in g])
            ev = 0
            for g in range(2):
                slot, wbuf = w_next()
                wv = wview(slot, (KC, 512))
                for t in range(16):
                    bank = next_bank()
                    fns = [(OP('matmul',
                        ps[:, bank, :], hT[:, kc, 128 * t:128 * (t + 1)], wv[:, kc, :],
                        start=(kc == 0), stop=(kc == KC - 1))) for kc in range(KC)]
                    S.emit("pe", fns, reads=[wbuf] + [hTb[kc][t // 4] for kc in range(KC)], writes=[pb[bank]])
                    dst = vaug[:, 4 * g:4 * g + 4, t, 0:128]
                    src = ps[:, bank, :].rearrange("p (h e) -> p h e", h=4)
                    if ev % 2 == 0:
                        S.emit("act", OP('activation', out=dst, in_=src, func=AF.Copy),
                               reads=[pb[bank]], writes=[vb[g][t]])
                    else:
                        S.emit("dve", OP('tensor_copy', out=dst, in_=src),
                               reads=[pb[bank]], writes=[vb[g][t]])
                    ev += 1
                w_prefetch()
            pk = [0]
            epk = [0]
            onk = [0]
            deferred = []

            def flush_deferred():
                while deferred:
                    deferred.pop(0)()

            for h in range(H):
                slope = 2.0 ** (-(h + 1))
                qscale = 1.0 / (8.0 * slope)
                slot, wbuf = w_next()
                wqk = wview(slot, (KC, 256))
                for tb in range(NB):
                    blk = slice(512 * tb, 512 * (tb + 1))
                    for which in range(2):
                        bank = next_bank()
                        fns = [(OP('matmul',
                            ps[:, bank, :], wqk[:, kc, 128 * which:128 * (which + 1)], hT[:, kc, blk],
                            start=(kc == 0), stop=(kc == KC - 1))) for kc in range(KC)]
                        S.emit("pe", fns, reads=[wbuf] + [hTb[kc][tb] for kc in range(KC)], writes=[pb[bank]])
                        if which == 0:
                            S.emit("act", OP('activation', out=QQ[:, blk], in_=ps[:, bank, :],
                                                                                  func=AF.Copy, scale=float(qscale)),
                                   reads=[pb[bank]], writes=[QQb[tb]])
                        else:
                            S.emit("dve", OP('tensor_copy', out=KK[:, blk], in_=ps[:, bank, :]),
                                   reads=[pb[bank]], writes=[KKb[tb]])
                w_prefetch()
                for qb in range(NB):
                    for kt in range(4 * qb + 4):
                        lo = max(128 * kt, 512 * qb)
                        hi = 512 * (qb + 1)
                        N = hi - lo
                        diag = kt >= 4 * qb
                        par = pk[0] % 2
                        pk[0] += 1
                        sb = 2 * par
                        fns = []
                        for m in range(2):
                            rs = slice(64 * m, 64 * (m + 1))
                            fns.append(OP('matmul',
                                ps[:, sb + m, 0:N], KK[rs, 128 * kt:128 * (kt + 1)], QQ[rs, lo:hi],
                                start=True, stop=False))
                        off0 = 128 if diag else 0
                        for m in range(2):
                            if diag:
                                fns.append(OP('matmul',
                                    ps[:, sb + m, 0:128], identb[:, :], maskd[:, :], start=False, stop=(N == 128)))
                            if N > off0:
                                fns.append(OP('matmul',
                                    ps[:, sb + m, off0:N], kaug[0:3, :], qaug[0:3, off0:N], start=False, stop=True))
                        S.emit("pe", fns, reads=[KKb[kt // 4], QQb[qb], cbuf], writes=[pb[sb], pb[sb + 1]])
                        jb = (lo - 128 * kt) // 128
                        bias_ap = btab[:, h * 16 + jb:h * 16 + jb + 1]
                        S.emit("act", OP('activation',
                            out=Pt[:, par, :, 0:N], in_=ps[:, sb:sb + 2, 0:N], func=AF.Exp,
                            scale=float(slope), bias=bias_ap),
                               reads=[pb[sb], pb[sb + 1], cbuf], writes=[Ptb[par]])
                        fns = []
                        wr = []
                        for j in range(N // 128):
                            qt = lo // 128 + j
                            A = 4 + qt % 4
                            for m in range(2):
                                fns.append(OP('matmul',
                                    ps[:, A, 130 * m:130 * m + 129], Pt[:, par, m, 128 * j:128 * (j + 1)],
                                    vaug[:, h, kt, 0:129], start=(kt == 0 and m == 0), stop=(kt == qt),
                                    skip_group_check=True))
                            wr.append(pb[A])
                        S.emit("pe", fns, reads=[Ptb[par], vb[h // 4][kt]], writes=wr)
                        if kt >= 2:
                            flush_deferred()
                        if diag:
                            qt = kt
                            A = 4 + qt % 4
                            ep = epk[0] % NEP
                            epk[0] += 1
                            o_i = onk[0] % NON
                            onk[0] += 1
                            eb = epb[ep]
                            S.emit("dve", OP('reciprocal', out=rl[:, ep, 0:2], in_=ps[:, A, 128:259:130]),
                                   reads=[pb[A]], writes=[eb])
                            S.emit("dve", OP('tensor_tensor', out=rl[:, ep, 2:3], in0=rl[:, ep, 1:2],
                                                                         in1=lamt[:, 5:6], op=ALU.mult),
                                   reads=[eb, cbuf], writes=[eb])
                            S.emit("dve", OP('tensor_scalar', out=t2[:, ep, :], in0=ps[:, A, 130:258],
                                                                             scalar1=rl[:, ep, 2:3], scalar2=None,
                                                                             op0=ALU.mult),
                                   reads=[pb[A], eb], writes=[eb])
                            S.emit("dve", OP('scalar_tensor_tensor',
                                out=ot[:, ep, :], in0=ps[:, A, 0:128], scalar=rl[:, ep, 0:1], in1=t2[:, ep, :],
                                op0=ALU.mult, op1=ALU.subtract), reads=[pb[A], eb], writes=[eb])
                            S.emit("dve", OP('scalar_tensor_tensor',
                                out=junk[:, ep, :], in0=ot[:, ep, :], scalar=1.0, in1=ot[:, ep, :],
                                op0=ALU.mult, op1=ALU.mult, accum_out=sst[:, ep, 0:1]), reads=[eb], writes=[eb])
                            S.emit("dve", OP('tensor_scalar', out=sst[:, ep, 1:2], in0=sst[:, ep, 0:1],
                                                                        scalar1=float(128.0 * SUBLN_EPS), scalar2=None,
                                                                        op0=ALU.add), reads=[eb], writes=[eb])
                            S.emit("pool", OP('tensor_tensor', out=sst[:, ep, 2:3], in0=sst[:, ep, 1:2],
                                                                          in1=mhalf[:, 0:1], op=ALU.pow),
                                   reads=[eb, cbuf], writes=[eb])
                            S.emit("dve", OP('scalar_tensor_tensor',
                                out=on[:, o_i, :], in0=ot[:, ep, :], scalar=sst[:, ep, 2:3], in1=gsub[:, :],
                                op0=ALU.mult, op1=ALU.mult), reads=[eb, cbuf], writes=[onb[o_i]])

                            def do_tr(qt=qt, o_i=o_i, qb=qb, h=h):
                                tbank = 0 if (qb % 2 == 0) else 2
                                psb = ps[:, tbank, :].bitcast(BF16)
                                jj = qt % 4
                                S.emit("pe", OP('transpose', psb[:, 0:128], on[:, o_i, :], identb[:, :]),
                                       reads=[onb[o_i], cbuf], writes=[pb[tbank]])
                                S.emit("dve", OP('tensor_copy', out=oT[:, h, 128 * qt:128 * (qt + 1)], in_=psb[:, 0:128]),
                                       reads=[pb[tbank]], writes=[oTb[h][qb]])

                            deferred.append(do_tr)
                flush_deferred()
            flush_deferred()
            S.barrier()
            AR.off = 0
            nt = NormTmp()
            pt = PostTmp()
            assert AR.off <= 8320
            proj_post(nt, pt, L * 4 + 1,
                      lambda kc, tb: oT[:, kc, 512 * tb:512 * (tb + 1)],
                      lambda tb: [oTb[kc][tb] for kc in range(KC)])

        setup()
        w_prefetch()
        phase_load_x()
        for L in layers:
            if 'dbg_g32' in parts:
                S.emit("dve", OP('tensor_copy', out=xT[:, 0, 0:64], in_=g32[:, :]), reads=[cbuf], writes=[xTb[0][0]])
                S.emit("dve", OP('tensor_copy', out=xT[:, 0, 64:328], in_=fcw[:, :]), reads=[cbuf], writes=[xTb[0][0]])
                S.emit("dve", OP('tensor_copy', out=xT[:, 0, 328:352], in_=cwt[:, :]), reads=[cbuf], writes=[xTb[0][0]])
                S.emit("dve", OP('tensor_copy', out=xT[:, 0, 352:360], in_=lamt[:, :]), reads=[cbuf], writes=[xTb[0][0]])
                S.emit("dve", OP('tensor_copy', out=xT[:, 0, 384:512], in_=gsub[:, :]), reads=[cbuf], writes=[xTb[0][0]])
            if 'dbg_prenorm' in parts:
                S.barrier()
                AR.reset()
                ntd = NormTmp()
                prenorm(ntd, L * 4 + 0, range(NB))
                for c in range(KC):
                    for tb in range(NB):
                        S.emit("dve", OP('tensor_copy', out=xT[:, c, 512 * tb:512 * (tb + 1)],
                                                                         in_=hT[:, c, 512 * tb:512 * (tb + 1)]),
                               reads=[hTb[c][tb]], writes=[xTb[c][tb]])
            if 'mix' in parts:
                S.barrier()
                if L == 0:
                    phase_attn(L)
                else:
                    phase_conv(L)
            if 'ffn' in parts:
                S.barrier()
                phase_ffn(L)
        S.barrier()
        phase_store_y()
        S.wait_all("sp", out_toks)

        sems = {}
        for k in list(S.cnt.keys()):
            nm = "s_" + "_".join(str(t) for t in (k if isinstance(k, tuple) else (k,)))
            sems[k] = E(nc.semaphore(nm))
        block = E(nc.Block())

        def replay(eng_name):
            def run(e):
                for (waits, fns, tok, inc) in S.ops[eng_name]:
                    for (k, v) in waits:
                        e.wait_ge(sems[k], v)
                    ins = None
                    for fn in fns:
                        ins = fn(e)
                    if tok is not None and ins is not None:
                        ins.then_inc(sems[tok[0]], inc)
            return run

        block.tensor(replay("pe"))
        block.scalar(replay("act"))
        block.vector(replay("dve"))
        block.gpsimd(replay("pool"))
        block.sync(replay("sp"))
    return nc


def make_consts():
    c = {}
    c["c_ident"] = np.eye(128, dtype=np.float32)
    c["c_ones"] = np.ones((128, 128), dtype=np.float32)
    p = np.arange(128, dtype=np.float32)
    c["c_kaug"] = np.stack([p, np.ones(128, np.float32), np.ones(128, np.float32)], 0).astype(np.float32)
    qr = np.arange(512)
    c["c_qaug"] = np.stack([np.ones(512), -(qr % 128), -128.0 * (qr // 128)], 0).astype(np.float32)
    k = np.arange(128)[:, None]
    q = np.arange(128)[None, :]
    allowed = (k // 64) <= (q // 64)
    c["c_maskd"] = np.where(allowed, -np.abs(q - k).astype(np.float32), np.float32(NEG_BIG)).astype(np.float32)
    bt = np.zeros((128, 128), np.float32)
    for h in range(H):
        slope = 2.0 ** (-(h + 1))
        for j in range(16):
            bt[:, h * 16 + j] = -128.0 * j * slope
    c["c_btab"] = bt
    return c


_NC_CACHE = {}


def kernel(x, norm_g, attn_w_qkv, attn_w_o, attn_lambda_q1, attn_lambda_k1, attn_lambda_q2,
           attn_lambda_k2, attn_subln_g, conv_w_in, conv_w, conv_w_out, ffn_w_up, ffn_conv_w, ffn_w_down):
    n = 8
    f = lambda a: np.ascontiguousarray(np.asarray(a, dtype=np.float32))
    shared = {
        "norm_g": f(norm_g), "attn_w_qkv": f(attn_w_qkv), "attn_w_o": f(attn_w_o),
        "attn_lambda_q1": f(attn_lambda_q1), "attn_lambda_k1": f(attn_lambda_k1),
        "attn_lambda_q2": f(attn_lambda_q2), "attn_lambda_k2": f(attn_lambda_k2),
        "attn_subln_g": f(attn_subln_g), "conv_w_in": f(conv_w_in), "conv_w": f(conv_w),
        "conv_w_out": f(conv_w_out), "ffn_w_up": f(ffn_w_up), "ffn_conv_w": f(ffn_conv_w),
        "ffn_w_down": f(ffn_w_down),
    }
    shared.update(make_consts())
    xs = f(x)
    nc = build_program((0, 1))
    in_maps = []
    for i in range(n):
        m = dict(shared)
        m["x"] = np.ascontiguousarray(xs[i])
        in_maps.append(m)
    res = run_bass_kernel_spmd(nc, in_maps, core_ids=list(range(n)))
    return np.stack([np.asarray(r["y"], dtype=np.float32) for r in res.results], axis=0)
```

```python
import math
from contextlib import ExitStack

import numpy as np
import concourse.bass as bass
import concourse.mybir as mybir
from concourse.bass_utils import run_bass_kernel_spmd

F32 = mybir.dt.float32
BF16 = mybir.dt.bfloat16
AF = mybir.ActivationFunctionType
ALU = mybir.AluOpType

S_LEN = 2048
D = 1024
NB = 4
KC = 8
DFF = 2816
NFI = 22
H = 8
NORM_EPS = 1e-6
SUBLN_EPS = 1e-5
NEG_BIG = -60000.0

ENGS = ("pe", "act", "dve", "pool", "sp")


def OP(name, *args, **kw):
    return lambda e: getattr(e, name)(*args, **kw)


class Buf:
    __slots__ = ("name", "w", "r")

    def __init__(self, name=""):
        self.name = name
        self.w = None
        self.r = {}


class Sched:
    def __init__(self):
        self.ops = {e: [] for e in ENGS}
        self.cnt = {}
        self.seen = {e: {} for e in ENGS}
        self.dma_keys = []

    def _waits(self, eng, reads, writes):
        deps = {}

        def add(tok):
            if tok is None:
                return
            k, v = tok
            if deps.get(k, 0) < v:
                deps[k] = v

        for b in reads:
            add(b.w)
        for b in writes:
            add(b.w)
            for k, v in b.r.items():
                add((k, v))
        waits = []
        for k, v in deps.items():
            if eng == "pe" and k == "pe":
                continue
            if self.seen[eng].get(k, 0) >= v:
                continue
            self.seen[eng][k] = v
            waits.append((k, v))
        return waits

    def _commit(self, tok, reads, writes):
        for b in reads:
            if b.r.get(tok[0], 0) < tok[1]:
                b.r[tok[0]] = tok[1]
        for b in writes:
            b.w = tok
            b.r = {}

    def emit(self, eng, fns, reads=(), writes=()):
        if not isinstance(fns, (list, tuple)):
            fns = [fns]
        waits = self._waits(eng, reads, writes)
        self.cnt[eng] = self.cnt.get(eng, 0) + 1
        tok = (eng, self.cnt[eng])
        self.ops[eng].append((waits, list(fns), tok, 1))
        self._commit(tok, reads, writes)
        return tok

    def dma(self, qeng, fn, key, reads=(), writes=(), nowait=False):
        waits = [] if nowait else self._waits(qeng, reads, writes)
        if key not in self.cnt:
            self.cnt[key] = 0
            self.dma_keys.append(key)
        self.cnt[key] += 16
        tok = (key, self.cnt[key])
        self.ops[qeng].append((waits, [fn], tok, 16))
        self._commit(tok, reads, writes)
        return tok

    def barrier(self):
        for e in ENGS:
            waits = []
            for k in ("pe", "act", "dve", "pool"):
                v = self.cnt.get(k, 0)
                if v == 0 or (e == k and e == "pe"):
                    continue
                if self.seen[e].get(k, 0) >= v:
                    continue
                self.seen[e][k] = v
                waits.append((k, v))
            if waits:
                self.ops[e].append((waits, [], None, 0))

    def wait_all(self, eng, toks):
        waits = []
        for k, v in toks:
            if self.seen[eng].get(k, 0) >= v:
                continue
            self.seen[eng][k] = v
            waits.append((k, v))
        if waits:
            self.ops[eng].append((waits, [], None, 0))


def build_program(layers=(0, 1), parts=('mix', 'ffn')):
    nc = bass.Bass("TRN2", target_bir_lowering=False)
    S = Sched()

    def din(name, shape):
        return nc.dram_tensor(name, list(shape), F32, kind="ExternalInput").ap()

    x_d = din("x", (S_LEN, D))
    norm_g_d = din("norm_g", (2, 4, D))
    wqkv_d = din("attn_w_qkv", (1, D, 3072))
    wo_d = din("attn_w_o", (1, D, D))
    lq1_d = din("attn_lambda_q1", (1, 64))
    lk1_d = din("attn_lambda_k1", (1, 64))
    lq2_d = din("attn_lambda_q2", (1, 64))
    lk2_d = din("attn_lambda_k2", (1, 64))
    subg_d = din("attn_subln_g", (1, 128))
    win_d = din("conv_w_in", (1, D, 3072))
    cw_d = din("conv_w", (1, 3, D))
    wout_d = din("conv_w_out", (1, D, D))
    wup_d = din("ffn_w_up", (2, D, 2 * DFF))
    fcw_d = din("ffn_conv_w", (2, 3, 2 * DFF))
    wdn_d = din("ffn_w_down", (2, DFF, D))
    ident_d = din("c_ident", (128, 128))
    ones_d = din("c_ones", (128, 128))
    kaug_d = din("c_kaug", (128, 128))
    qaug_d = din("c_qaug", (128, 512))
    maskd_d = din("c_maskd", (128, 128))
    btab_d = din("c_btab", (128, 128))
    y_d = nc.dram_tensor("y", [S_LEN, D], F32, kind="ExternalOutput").ap()

    es = ExitStack()
    with es:
        E = es.enter_context
        xT = E(nc.sbuf_tensor("xT", [128, KC, S_LEN], F32))
        hT = E(nc.sbuf_tensor("hT", [128, KC, S_LEN], BF16))
        NSLOT = 3
        wring = E(nc.sbuf_tensor("wring", [128, NSLOT, 4096], BF16))
        identf = E(nc.sbuf_tensor("identf", [128, 128], F32))
        identb = E(nc.sbuf_tensor("identb", [128, 128], BF16))
        onesb = E(nc.sbuf_tensor("onesb", [128, 128], BF16))
        kaug = E(nc.sbuf_tensor("kaug", [128, 128], BF16))
        qaug = E(nc.sbuf_tensor("qaug", [128, 512], BF16))
        maskd = E(nc.sbuf_tensor("maskd", [128, 128], BF16))
        btab = E(nc.sbuf_tensor("btab", [128, 128], F32))
        mhalf = E(nc.sbuf_tensor("mhalf", [128, 2], F32))
        epst = E(nc.sbuf_tensor("epst", [128, 2], F32))
        g32 = E(nc.sbuf_tensor("g32", [128, 64], F32))
        fcw = E(nc.sbuf_tensor("fcw", [128, 264], F32))
        cwt = E(nc.sbuf_tensor("cwt", [128, 24], F32))
        gsub = E(nc.sbuf_tensor("gsub", [128, 128], F32))
        lamt = E(nc.sbuf_tensor("lamt", [128, 8], F32))
        pstage = E(nc.sbuf_tensor("pstage", [128, 128], F32))
        lstage = E(nc.sbuf_tensor("lstage", [128, 4, 64], F32))
        ARENA_W = 20300
        arena = E(nc.sbuf_tensor("arena", [128, ARENA_W], F32))
        ps = E(nc.psum_tensor("ps", [128, 8, 512], F32))
        pb = [Buf("pb%d" % i) for i in range(8)]

        class Arena:
            def __init__(self):
                self.off = 0

            def reset(self):
                self.off = 0

            def f32(self, n):
                a = arena[:, self.off:self.off + n]
                self.off += n
                assert self.off <= ARENA_W, self.off
                return a

            def bf16(self, n):
                w = (n + 1) // 2
                a = arena[:, self.off:self.off + w].bitcast(BF16)
                self.off += w
                assert self.off <= ARENA_W, self.off
                return a

        AR = Arena()

        xTb = [[Buf("xT%d_%d" % (c, tb)) for tb in range(NB)] for c in range(KC)]
        hTb = [[Buf("hT%d_%d" % (c, tb)) for tb in range(NB)] for c in range(KC)]
        cbuf = Buf("consts")

        bank_rr = [0]

        def next_bank(lo=0, n=4):
            b = lo + bank_rr[0] % n
            bank_rr[0] += 1
            return b

        wslot_buf = [Buf("wslot%d" % i) for i in range(NSLOT)]
        wplan = []
        wstate = {"issued": 0, "taken": 0}

        def w_issue_upto(n):
            while wstate["issued"] < min(n, len(wplan)):
                i = wstate["issued"]
                slot = i % NSLOT
                for pi, (dst_fn, src) in enumerate(wplan[i]):
                    dst = dst_fn(slot)
                    S.dma("pool", OP('dma_start', out=dst, in_=src),
                          ("w", slot), writes=[wslot_buf[slot]], nowait=(pi > 0))
                wstate["issued"] += 1

        def w_next():
            i = wstate["taken"]
            w_issue_upto(i + 1)
            wstate["taken"] += 1
            return i % NSLOT, wslot_buf[i % NSLOT]

        def w_prefetch():
            w_issue_upto(wstate["taken"] + NSLOT)

        def wview(slot, dims):
            n = int(np.prod(dims))
            a = wring[:, slot, 0:n]
            if len(dims) == 2:
                return a.rearrange("p (a b) -> p a b", a=dims[0])
            if len(dims) == 3:
                return a.rearrange("p (a b c) -> p a b c", a=dims[0], b=dims[1])
            return a

        def plan_weights():
            def rows(wd):
                return wd.rearrange("(kc p) n -> p kc n", p=128)

            for L in layers:
                if 'mix' not in parts:
                    pass
                elif L == 0:
                    w = rows(wqkv_d[0])
                    for g in range(2):
                        wplan.append([(lambda s: wview(s, (KC, 512)), w[:, :, 2048 + 512 * g:2048 + 512 * (g + 1)])])
                    for h in range(H):
                        ld = []
                        for j, base in enumerate((0, 512, 1024, 1536)):
                            ld.append((lambda s, j=j: wview(s, (KC, 256))[:, :, 64 * j:64 * (j + 1)],
                                       w[:, :, base + 64 * h:base + 64 * (h + 1)]))
                        wplan.append(ld)
                    w = rows(wo_d[0])
                    for g in range(2):
                        wplan.append([(lambda s: wview(s, (KC, 512)), w[:, :, 512 * g:512 * (g + 1)])])
                else:
                    w = rows(win_d[0])
                    for j in range(KC):
                        ld = []
                        for sec in range(3):
                            ld.append((lambda s, sec=sec: wview(s, (KC, 3, 128))[:, :, sec, :],
                                       w[:, :, sec * 1024 + 128 * j:sec * 1024 + 128 * (j + 1)]))
                        wplan.append(ld)
                    w = rows(wout_d[0])
                    for g in range(2):
                        wplan.append([(lambda s: wview(s, (KC, 512)), w[:, :, 512 * g:512 * (g + 1)])])
                wu = rows(wup_d[L])
                wd = wdn_d[L].rearrange("(fi p) n -> p fi n", p=128)
                for tb in range(NB if 'ffn' in parts else 0):
                    for fp in range(NFI // 2):
                        ld = []
                        for gv in range(2):
                            ld.append((lambda s, gv=gv: wview(s, (2, KC, 256))[:, gv, :, :],
                                       wu[:, :, gv * DFF + 256 * fp:gv * DFF + 256 * (fp + 1)]))
                        wplan.append(ld)
                    for c in range(KC):
                        wplan.append([(lambda s: wview(s, (NFI, 128)), wd[:, :, 128 * c:128 * (c + 1)])])

        plan_weights()

        def setup():
            def cload(q, dst, src, key):
                S.dma(q, OP('dma_start', out=dst, in_=src), key, writes=[cbuf])

            cload("sp", identf[:, :], ident_d[:, :], "c0")
            cload("sp", btab[:, :], btab_d[:, :], "c0")
            cload("pool", identb[:, :], ident_d[:, :], "c1")
            cload("pool", onesb[:, :], ones_d[:, :], "c1")
            cload("pool", kaug[:, :], kaug_d[:, :], "c1")
            cload("pool", qaug[:, :], qaug_d[:, :], "c1")
            cload("pool", maskd[:, :], maskd_d[:, :], "c1")
            S.emit("pool", OP('memset', mhalf[:, :], -0.5), writes=[cbuf])
            S.emit("pool", OP('memset', epst[:, :], float(D * NORM_EPS)), writes=[cbuf])
            pst = Buf("pstage")

            def tr_param(src2d, nrows, dst, scale):
                S.dma("sp", (OP('dma_start', out=pstage[0:nrows, :], in_=src2d)), "c2", writes=[pst])
                bank = next_bank()
                S.emit("pe", OP('transpose', ps[:, bank, 0:nrows], pstage[0:nrows, :], identf[0:nrows, 0:nrows]),
                       reads=[pst, cbuf], writes=[pb[bank]])
                S.emit("dve", OP('tensor_scalar', out=dst, in0=ps[:, bank, 0:nrows], scalar1=float(scale),
                                                       scalar2=None, op0=ALU.mult),
                       reads=[pb[bank]], writes=[cbuf])

            tr_param(norm_g_d.rearrange("l i (c p) -> (l i c) p", p=128), 64, g32[:, :], 32.0)
            f2 = fcw_d.rearrange("l j (c p) -> (l j c) p", p=128)
            for i in range(3):
                tr_param(f2[88 * i:88 * (i + 1), :], 88, fcw[:, 88 * i:88 * (i + 1)], 1.0)
            tr_param(cw_d.rearrange("l j (c p) -> (l j c) p", p=128), 24, cwt[:, :], 1.0)
            lam_init = 0.8 - 0.6 * math.exp(-0.3 * 0)
            S.dma("sp", (OP('dma_start', out=gsub[:, :], in_=subg_d[0:1, :].broadcast_to([128, 128]))),
                  "c3", writes=[cbuf])
            S.emit("dve", OP('tensor_scalar', out=gsub[:, :], in0=gsub[:, :],
                                                   scalar1=float((1.0 - lam_init) * math.sqrt(128.0)),
                                                   scalar2=None, op0=ALU.mult), reads=[cbuf], writes=[cbuf])
            for i, ldd in enumerate((lq1_d, lk1_d, lq2_d, lk2_d)):
                S.dma("sp", (OP('dma_start', out=lstage[:, i, :],
                                                                 in_=ldd[0:1, :].broadcast_to([128, 64]))),
                      "c3", writes=[cbuf])
            S.emit("dve", OP('tensor_tensor', out=lstage[:, 0, :], in0=lstage[:, 0, :], in1=lstage[:, 1, :],
                                                   op=ALU.mult), reads=[cbuf], writes=[cbuf])
            S.emit("dve", OP('tensor_tensor', out=lstage[:, 2, :], in0=lstage[:, 2, :], in1=lstage[:, 3, :],
                                                   op=ALU.mult), reads=[cbuf], writes=[cbuf])
            S.emit("dve", OP('reduce_sum', out=lamt[:, 0:1], in_=lstage[:, 0, :], axis=mybir.AxisListType.X),
                   reads=[cbuf], writes=[cbuf])
            S.emit("dve", OP('reduce_sum', out=lamt[:, 1:2], in_=lstage[:, 2, :], axis=mybir.AxisListType.X),
                   reads=[cbuf], writes=[cbuf])
            S.emit("act", OP('activation', out=lamt[:, 2:4], in_=lamt[:, 0:2], func=AF.Exp),
                   reads=[cbuf], writes=[cbuf])
            S.emit("dve", OP('tensor_tensor', out=lamt[:, 4:5], in0=lamt[:, 2:3], in1=lamt[:, 3:4],
                                                   op=ALU.subtract), reads=[cbuf], writes=[cbuf])
            S.emit("dve", OP('tensor_scalar', out=lamt[:, 5:6], in0=lamt[:, 4:5], scalar1=float(lam_init),
                                                   scalar2=None, op0=ALU.add), reads=[cbuf], writes=[cbuf])

        def phase_load_x():
            AR.reset()
            xs = AR.f32(8 * 1024).rearrange("p (s n) -> p s n", s=8)
            xsb = [Buf("xs%d" % i) for i in range(8)]

            def load(t):
                sl = t % 8
                S.dma("sp", (OP('dma_start', out=xs[:, sl, :], in_=x_d[128 * t:128 * (t + 1), :])),
                      ("xs", sl), writes=[xsb[sl]])

            for t in range(8):
                load(t)
            for tb in range(NB):
                for c in range(KC):
                    bank = next_bank()
                    fns = []
                    for j in range(4):
                        sl = (4 * tb + j) % 8
                        fns.append(OP('transpose',
                            ps[:, bank, 128 * j:128 * (j + 1)], xs[:, sl, 128 * c:128 * (c + 1)], identf[:, :]))
                    S.emit("pe", fns, reads=[xsb[(4 * tb + j) % 8] for j in range(4)] + [cbuf], writes=[pb[bank]])
                    dst = xT[:, c, 512 * tb:512 * (tb + 1)]
                    if c % 2 == 0:
                        S.emit("act", OP('activation', out=dst, in_=ps[:, bank, :], func=AF.Copy),
                               reads=[pb[bank]], writes=[xTb[c][tb]])
                    else:
                        S.emit("dve", OP('tensor_copy', out=dst, in_=ps[:, bank, :]),
                               reads=[pb[bank]], writes=[xTb[c][tb]])
                if tb + 2 < NB:
                    for j in range(4):
                        load(4 * (tb + 2) + j)

        out_toks = []

        def phase_store_y():
            AR.reset()
            ys = AR.f32(4 * 1024).rearrange("p (s n) -> p s n", s=4)
            ysb = [Buf("ys%d" % i) for i in range(4)]
            for t in range(16):
                tb = t // 4
                sl = t % 4
                for half in range(2):
                    bank = next_bank()
                    fns = []
                    for j in range(4):
                        c = 4 * half + j
                        fns.append(OP('transpose',
                            ps[:, bank, 128 * j:128 * (j + 1)], xT[:, c, 128 * t:128 * (t + 1)], identf[:, :]))
                    S.emit("pe", fns, reads=[xTb[4 * half + j][tb] for j in range(4)] + [cbuf], writes=[pb[bank]])
                    dst = ys[:, sl, 512 * half:512 * (half + 1)]
                    if half == 0:
                        S.emit("act", OP('activation', out=dst, in_=ps[:, bank, :], func=AF.Copy),
                               reads=[pb[bank]], writes=[ysb[sl]])
                    else:
                        S.emit("dve", OP('tensor_copy', out=dst, in_=ps[:, bank, :]),
                               reads=[pb[bank]], writes=[ysb[sl]])
                tok = S.dma("sp", (OP('dma_start', out=y_d[128 * t:128 * (t + 1), :], in_=ys[:, sl, :])),
                            ("ys", sl), reads=[ysb[sl]])
                out_toks.append(tok)

        class NormTmp:
            def __init__(self):
                self.sq = AR.bf16(4 * 512).rearrange("p (s n) -> p s n", s=4)
                self.sqb = [Buf("sq%d" % i) for i in range(4)]
                self.t1 = AR.f32(2 * 512).rearrange("p (s n) -> p s n", s=2)
                self.t1b = [Buf("t1_%d" % i) for i in range(2)]
                self.rstd = AR.f32(2 * 512).rearrange("p (s n) -> p s n", s=2)
                self.rstdb = [Buf("rstd%d" % i) for i in range(2)]
                self.k = 0
                self.r = 0

            def next_sq(self):
                i = self.k % 4
                self.k += 1
                return i

            def next_r(self):
                i = self.r % 2
                self.r += 1
                return i

        def emit_rstd(nt, ssbank, eps_total):
            r = nt.next_r()
            S.emit("act", OP('activation', out=nt.t1[:, r, :], in_=ps[:, ssbank, :], func=AF.Ln, bias=epst[:, 0:1]),
                   reads=[pb[ssbank], cbuf], writes=[nt.t1b[r]])
            S.emit("act", OP('activation', out=nt.rstd[:, r, :], in_=nt.t1[:, r, :], func=AF.Exp, scale=-0.5),
                   reads=[nt.t1b[r]], writes=[nt.rstdb[r]])
            return r

        def prenorm(nt, gidx, tbs):
            for tb in tbs:
                blk = slice(512 * tb, 512 * (tb + 1))
                ssbank = 7
                for c in range(KC):
                    i = nt.next_sq()
                    S.emit("act", OP('activation', out=nt.sq[:, i, :], in_=xT[:, c, blk], func=AF.Square),
                           reads=[xTb[c][tb]], writes=[nt.sqb[i]])
                    S.emit("pe", OP('matmul', ps[:, ssbank, :], onesb[:, :], nt.sq[:, i, :],
                                                             start=(c == 0), stop=(c == KC - 1)),
                           reads=[nt.sqb[i], cbuf], writes=[pb[ssbank]])
                r = emit_rstd(nt, ssbank, D * NORM_EPS)
                for c in range(KC):
                    S.emit("dve", OP('scalar_tensor_tensor',
                        out=hT[:, c, blk], in0=xT[:, c, blk], scalar=g32[:, gidx * 8 + c:gidx * 8 + c + 1],
                        in1=nt.rstd[:, r, :], op0=ALU.mult, op1=ALU.mult),
                           reads=[xTb[c][tb], nt.rstdb[r], cbuf], writes=[hTb[c][tb]])

        class PostTmp:
            def __init__(self):
                self.msb = AR.f32(KC * 512).rearrange("p (c n) -> p c n", c=KC)
                self.msbb = [Buf("msb%d" % i) for i in range(KC)]
                self.tp = AR.f32(2 * 512).rearrange("p (s n) -> p s n", s=2)
                self.tpb = [Buf("tp%d" % i) for i in range(2)]
                self.k = 0

        def post_block(nt, pt, gidx, tb, mm_group):
            blk = slice(512 * tb, 512 * (tb + 1))
            ssbank = 7
            pending = None

            def ss_mm(c, i):
                S.emit("pe", OP('matmul', ps[:, ssbank, :], onesb[:, :], nt.sq[:, i, :],
                                                start=(c == 0), stop=(c == KC - 1)),
                       reads=[nt.sqb[i], cbuf], writes=[pb[ssbank]])

            for c in range(KC):
                bank = next_bank()
                mm_group(c, bank)
                if pending is not None:
                    ss_mm(*pending)
                S.emit("act", OP('activation', out=pt.msb[:, c, :], in_=ps[:, bank, :], func=AF.Copy),
                       reads=[pb[bank]], writes=[pt.msbb[c]])
                i = nt.next_sq()
                S.emit("act", OP('activation', out=nt.sq[:, i, :], in_=pt.msb[:, c, :], func=AF.Square),
                       reads=[pt.msbb[c]], writes=[nt.sqb[i]])
                pending = (c, i)
            ss_mm(*pending)
            r = emit_rstd(nt, ssbank, D * NORM_EPS)
            for c in range(KC):
                j = pt.k % 2
                pt.k += 1
                S.emit("dve", OP('scalar_tensor_tensor',
                    out=pt.tp[:, j, :], in0=pt.msb[:, c, :], scalar=g32[:, gidx * 8 + c:gidx * 8 + c + 1],
                    in1=nt.rstd[:, r, :], op0=ALU.mult, op1=ALU.mult),
                       reads=[pt.msbb[c], nt.rstdb[r], cbuf], writes=[pt.tpb[j]])
                S.emit("pool", OP('tensor_tensor', out=xT[:, c, blk], in0=xT[:, c, blk], in1=pt.tp[:, j, :],
                                                                  op=ALU.add),
                       reads=[pt.tpb[j], xTb[c][tb]], writes=[xTb[c][tb]])

        def proj_post(nt, pt, gidx, in_ap, in_bufs):
            s0, b0 = w_next()
            s1, b1 = w_next()
            wt = [wview(s0, (KC, 512)), wview(s1, (KC, 512))]
            wb = [b0, b1]
            for tb in range(NB):
                def grp(c, bank, tb=tb):
                    w = wt[c // 4]
                    col = 128 * (c % 4)
                    fns = [(OP('matmul', ps[:, bank, :], w[:, kc, col:col + 128], in_ap(kc, tb),
                                                      start=(kc == 0), stop=(kc == KC - 1))) for kc in range(KC)]
                    S.emit("pe", fns, reads=[wb[c // 4]] + in_bufs(tb), writes=[pb[bank]])

                post_block(nt, pt, gidx, tb, grp)
            w_prefetch()

        def phase_ffn(L):
            AR.reset()
            nt = NormTmp()
            pt = PostTmp()
            aT = AR.bf16(NFI * 512).rearrange("p (f n) -> p f n", f=NFI)
            aTb = [Buf("aT%d" % i) for i in range(NFI)]
            NU = 4
            ug = AR.f32(NU * 516).rearrange("p (s n) -> p s n", s=NU)
            ugb = [Buf("ug%d" % i) for i in range(NU)]
            tcv = AR.f32(NU * 512).rearrange("p (s n) -> p s n", s=NU)
            tcb = [Buf("tc%d" % i) for i in range(NU)]
            sg = AR.bf16(2 * 512).rearrange("p (s n) -> p s n", s=2)
            sgb = [Buf("sg%d" % i) for i in range(2)]
            halo = AR.f32(44 * 2).rearrange("p (c n) -> p c n", c=44)
            halob = [Buf("halo%d" % i) for i in range(44)]
            uk = [0]
            sk = [0]
            for tb in range(NB):
                blk = slice(512 * tb, 512 * (tb + 1))
                prenorm(nt, L * 4 + 2, [tb])
                for fp in range(NFI // 2):
                    slot, wbuf = w_next()
                    wt = wview(slot, (2, KC, 256))
                    for f2 in range(2):
                        fi = 2 * fp + f2
                        tcs = []
                        for gv in range(2):
                            ch = gv * NFI + fi
                            bank = next_bank()
                            fns = [(OP('matmul',
                                ps[:, bank, :], wt[:, gv, kc, 128 * f2:128 * (f2 + 1)], hT[:, kc, blk],
                                start=(kc == 0), stop=(kc == KC - 1))) for kc in range(KC)]
                            S.emit("pe", fns, reads=[wbuf] + [hTb[kc][tb] for kc in range(KC)], writes=[pb[bank]])
                            u = uk[0] % NU
                            uk[0] += 1
                            if tb == 0:
                                S.emit("pool", OP('memset', ug[:, u, 0:2], 0.0), writes=[ugb[u]])
                            else:
                                S.emit("pool", OP('tensor_copy', out=ug[:, u, 0:2], in_=halo[:, ch, :]),
                                       reads=[halob[ch]], writes=[ugb[u]])
                            S.emit("act", OP('activation', out=ug[:, u, 2:514], in_=ps[:, bank, :],
                                                                              func=AF.Copy),
                                   reads=[pb[bank]], writes=[ugb[u]])
                            if tb < NB - 1:
                                S.emit("pool", OP('tensor_copy', out=halo[:, ch, :], in_=ug[:, u, 512:514]),
                                       reads=[ugb[u]], writes=[halob[ch]])
                            wi = lambda j, ch=ch: fcw[:, (L * 3 + j) * 44 + ch:(L * 3 + j) * 44 + ch + 1]
                            S.emit("dve", OP('tensor_scalar', out=tcv[:, u, :], in0=ug[:, u, 2:514],
                                                                             scalar1=wi(2), scalar2=None, op0=ALU.mult),
                                   reads=[ugb[u], cbuf], writes=[tcb[u]])
                            S.emit("dve", OP('scalar_tensor_tensor',
                                out=tcv[:, u, :], in0=ug[:, u, 1:513], scalar=wi(1), in1=tcv[:, u, :],
                                op0=ALU.mult, op1=ALU.add), reads=[ugb[u], tcb[u], cbuf], writes=[tcb[u]])
                            S.emit("dve", OP('scalar_tensor_tensor',
                                out=tcv[:, u, :], in0=ug[:, u, 0:512], scalar=wi(0), in1=tcv[:, u, :],
                                op0=ALU.mult, op1=ALU.add), reads=[ugb[u], tcb[u], cbuf], writes=[tcb[u]])
                            tcs.append(u)
                        s = sk[0] % 2
                        sk[0] += 1
                        S.emit("act", OP('activation', out=sg[:, s, :], in_=tcv[:, tcs[0], :], func=AF.Silu),
                               reads=[tcb[tcs[0]]], writes=[sgb[s]])
                        S.emit("dve", OP('tensor_tensor', out=aT[:, fi, :], in0=sg[:, s, :], in1=tcv[:, tcs[1], :], op=ALU.mult),
                               reads=[sgb[s], tcb[tcs[1]]], writes=[aTb[fi]])
                    w_prefetch()
                wds = {}

                def grp(c, bank):
                    slot, wbuf = w_next()
                    wd = wview(slot, (NFI, 128))
                    fns = [(OP('matmul', ps[:, bank, :], wd[:, fi, :], aT[:, fi, :],
                                                      start=(fi == 0), stop=(fi == NFI - 1))) for fi in range(NFI)]
                    S.emit("pe", fns, reads=[wbuf] + aTb, writes=[pb[bank]])
                    w_prefetch()

                post_block(nt, pt, L * 4 + 3, tb, grp)

        def phase_conv(L):
            AR.reset()
            nt = NormTmp()
            pt = PostTmp()
            gT = AR.bf16(KC * S_LEN).rearrange("p (c n) -> p c n", c=KC)
            gTb = [[Buf("gT%d_%d" % (c, tb)) for tb in range(NB)] for c in range(KC)]
            uc = AR.f32(2 * 512).rearrange("p (s n) -> p s n", s=2)
            ucb = [Buf("uc%d" % i) for i in range(2)]
            z = AR.f32(2 * 516).rearrange("p (s n) -> p s n", s=2)
            zb = [Buf("z%d" % i) for i in range(2)]
            tcv = AR.f32(2 * 512).rearrange("p (s n) -> p s n", s=2)
            tcb = [Buf("tcc%d" % i) for i in range(2)]
            prenorm(nt, L * 4 + 0, range(NB))
            k = 0
            for j in range(KC):
                slot, wbuf = w_next()
                wt = wview(slot, (KC, 3, 128))
                for tb in range(NB):
                    blk = slice(512 * tb, 512 * (tb + 1))
                    banks = []
                    for sec in range(3):
                        bank = next_bank(0, 6)
                        fns = [(OP('matmul',
                            ps[:, bank, :], wt[:, kc, sec, :], hT[:, kc, blk],
                            start=(kc == 0), stop=(kc == KC - 1))) for kc in range(KC)]
                        S.emit("pe", fns, reads=[wbuf] + [hTb[kc][tb] for kc in range(KC)], writes=[pb[bank]])
                        banks.append(bank)
                    bb, bc, bh = banks
                    r = k % 2
                    k += 1
                    S.emit("act", OP('activation', out=uc[:, r, :], in_=ps[:, bc, :], func=AF.Copy),
                           reads=[pb[bc]], writes=[ucb[r]])
                    if tb == 0:
                        S.emit("pool", OP('memset', z[:, r, 0:2], 0.0), writes=[zb[r]])
                    else:
                        S.emit("pool", OP('tensor_copy', out=z[:, r, 0:2], in_=z[:, 1 - r, 512:514]),
                               reads=[zb[1 - r]], writes=[zb[r]])
                    S.emit("dve", OP('tensor_tensor', out=z[:, r, 2:514], in0=uc[:, r, :], in1=ps[:, bh, :],
                                                                      op=ALU.mult),
                           reads=[ucb[r], pb[bh]], writes=[zb[r]])
                    wi = lambda jj, j=j: cwt[:, jj * 8 + j:jj * 8 + j + 1]
                    S.emit("dve", OP('tensor_scalar', out=tcv[:, r, :], in0=z[:, r, 2:514], scalar1=wi(2),
                                                                     scalar2=None, op0=ALU.mult),
                           reads=[zb[r], cbuf], writes=[tcb[r]])
                    S.emit("dve", OP('scalar_tensor_tensor',
                        out=tcv[:, r, :], in0=z[:, r, 1:513], scalar=wi(1), in1=tcv[:, r, :], op0=ALU.mult, op1=ALU.add),
                           reads=[zb[r], tcb[r], cbuf], writes=[tcb[r]])
                    S.emit("dve", OP('scalar_tensor_tensor',
                        out=tcv[:, r, :], in0=z[:, r, 0:512], scalar=wi(0), in1=tcv[:, r, :], op0=ALU.mult, op1=ALU.add),
                           reads=[zb[r], tcb[r], cbuf], writes=[tcb[r]])
                    S.emit("dve", OP('tensor_tensor',
                        out=gT[:, j, blk], in0=tcv[:, r, :], in1=ps[:, bb, :], op=ALU.mult),
                           reads=[tcb[r], pb[bb]], writes=[gTb[j][tb]])
                w_prefetch()
            proj_post(nt, pt, L * 4 + 1,
                      lambda kc, tb: gT[:, kc, 512 * tb:512 * (tb + 1)],
                      lambda tb: [gTb[kc][tb] for kc in range(KC)])

        def phase_attn(L):
            AR.reset()
            vaug = AR.bf16(H * 16 * 130).rearrange("p (h t e) -> p h t e", h=H, t=16)
            vb = [[Buf("v%d_%d" % (g, t)) for t in range(16)] for g in range(2)]
            oT = AR.bf16(H * S_LEN).rearrange("p (h n) -> p h n", h=H)
            oTb = [[Buf("oT%d_%d" % (h, tb)) for tb in range(NB)] for h in range(H)]
            QQ = AR.bf16(S_LEN)
            KK = AR.bf16(S_LEN)
            QQb = [Buf("QQ%d" % tb) for tb in range(NB)]
            KKb = [Buf("KK%d" % tb) for tb in range(NB)]
            Pt = AR.bf16(2 * 2 * 512).rearrange("p (s m n) -> p s m n", s=2, m=2)
            Ptb = [Buf("Pt%d" % i) for i in range(2)]
            NEP = 2
            rl = AR.f32(NEP * 4).rearrange("p (s n) -> p s n", s=NEP)
            t2 = AR.f32(NEP * 128).rearrange("p (s n) -> p s n", s=NEP)
            ot = AR.f32(NEP * 128).rearrange("p (s n) -> p s n", s=NEP)
            junk = t2
            sst = AR.f32(NEP * 4).rearrange("p (s n) -> p s n", s=NEP)
            epb = [Buf("ep%d" % i) for i in range(NEP)]
            NON = 2
            on = AR.bf16(NON * 128).rearrange("p (s n) -> p s n", s=NON)
            onb = [Buf("on%d" % i) for i in range(NON)]
            AR.off = 8320
            nt0 = NormTmp()
            prenorm(nt0, L * 4 + 0, range(NB))
            S.barrier()
            S.emit("pool", OP('memset', vaug[:, :, :, 128:129], 1.0), writes=[b for g in vb for b in g])
            ev = 0
            for g in range(2):
                slot, wbuf = w_next()
                wv = wview(slot, (KC, 512))
                for t in range(16):
                    bank = next_bank()
                    fns = [(OP('matmul',
                        ps[:, bank, :], hT[:, kc, 128 * t:128 * (t + 1)], wv[:, kc, :],
                        start=(kc == 0), stop=(kc == KC - 1))) for kc in range(KC)]
                    S.emit("pe", fns, reads=[wbuf] + [hTb[kc][t // 4] for kc in range(KC)], writes=[pb[bank]])
                    dst = vaug[:, 4 * g:4 * g + 4, t, 0:128]
                    src = ps[:, bank, :].rearrange("p (h e) -> p h e", h=4)
                    if ev % 2 == 0:
                        S.emit("act", OP('activation', out=dst, in_=src, func=AF.Copy),
                               reads=[pb[bank]], writes=[vb[g][t]])
                    else:
                        S.emit("dve", OP('tensor_copy', out=dst, in_=src),
                               reads=[pb[bank]], writes=[vb[g][t]])
                    ev += 1
                w_prefetch()
            pk = [0]
            epk = [0]
            onk = [0]
            deferred = []

            def flush_deferred():
                while deferred:
                    deferred.pop(0)()

            for h in range(H):
                slope = 2.0 ** (-(h + 1))
                qscale = 1.0 / (8.0 * slope)
                slot, wbuf = w_next()
                wqk = wview(slot, (KC, 256))
                for tb in range(NB):
                    blk = slice(512 * tb, 512 * (tb + 1))
                    for which in range(2):
                        bank = next_bank()
                        fns = [(OP('matmul',
                            ps[:, bank, :], wqk[:, kc, 128 * which:128 * (which + 1)], hT[:, kc, blk],
                            start=(kc == 0), stop=(kc == KC - 1))) for kc in range(KC)]
                        S.emit("pe", fns, reads=[wbuf] + [hTb[kc][tb] for kc in range(KC)], writes=[pb[bank]])
                        if which == 0:
                            S.emit("act", OP('activation', out=QQ[:, blk], in_=ps[:, bank, :],
                                                                                  func=AF.Copy, scale=float(qscale)),
                                   reads=[pb[bank]], writes=[QQb[tb]])
                        else:
                            S.emit("dve", OP('tensor_copy', out=KK[:, blk], in_=ps[:, bank, :]),
                                   reads=[pb[bank]], writes=[KKb[tb]])
                w_prefetch()
                for qb in range(NB):
                    for kt in range(4 * qb + 4):
                        lo = max(128 * kt, 512 * qb)
                        hi = 512 * (qb + 1)
                        N = hi - lo
                        diag = kt >= 4 * qb
                        par = pk[0] % 2
                        pk[0] += 1
                        sb = 2 * par
                        fns = []
                        for m in range(2):
                            rs = slice(64 * m, 64 * (m + 1))
                            fns.append(OP('matmul',
                                ps[:, sb + m, 0:N], KK[rs, 128 * kt:128 * (kt + 1)], QQ[rs, lo:hi],
                                start=True, stop=False))
                        off0 = 128 if diag else 0
                        for m in range(2):
                            if diag:
                                fns.append(OP('matmul',
                                    ps[:, sb + m, 0:128], identb[:, :], maskd[:, :], start=False, stop=(N == 128)))
                            if N > off0:
                                fns.append(OP('matmul',
                                    ps[:, sb + m, off0:N], kaug[:, :], qaug[:, off0:N], start=False, stop=True))
                        S.emit("pe", fns, reads=[KKb[kt // 4], QQb[qb], cbuf], writes=[pb[sb], pb[sb + 1]])
                        jb = (lo - 128 * kt) // 128
                        bias_ap = btab[:, h * 16 + jb:h * 16 + jb + 1]
                        S.emit("act", OP('activation',
                            out=Pt[:, par, :, 0:N], in_=ps[:, sb:sb + 2, 0:N], func=AF.Exp,
                            scale=float(slope), bias=bias_ap),
                               reads=[pb[sb], pb[sb + 1], cbuf], writes=[Ptb[par]])
                        fns = []
                        wr = []
                        for j in range(N // 128):
                            qt = lo // 128 + j
                            A = 4 + qt % 4
                            for m in range(2):
                                fns.append(OP('matmul',
                                    ps[:, A, 130 * m:130 * m + 129], Pt[:, par, m, 128 * j:128 * (j + 1)],
                                    vaug[:, h, kt, 0:129], start=(kt == 0 and m == 0), stop=(kt == qt),
                                    skip_group_check=True))
                            wr.append(pb[A])
                        S.emit("pe", fns, reads=[Ptb[par], vb[h // 4][kt]], writes=wr)
                        if kt >= 2:
                            flush_deferred()
                        if diag:
                            qt = kt
                            A = 4 + qt % 4
                            ep = epk[0] % NEP
                            epk[0] += 1
                            o_i = onk[0] % NON
                            onk[0] += 1
                            eb = epb[ep]
                            S.emit("dve", OP('reciprocal', out=rl[:, ep, 0:2], in_=ps[:, A, 128:259:130]),
                                   reads=[pb[A]], writes=[eb])
                            S.emit("dve", OP('tensor_tensor', out=rl[:, ep, 2:3], in0=rl[:, ep, 1:2],
                                                                         in1=lamt[:, 5:6], op=ALU.mult),
                                   reads=[eb, cbuf], writes=[eb])
                            S.emit("dve", OP('tensor_scalar', out=t2[:, ep, :], in0=ps[:, A, 130:258],
                                                                             scalar1=rl[:, ep, 2:3], scalar2=None,
                                                                             op0=ALU.mult),
                                   reads=[pb[A], eb], writes=[eb])
                            S.emit("dve", OP('scalar_tensor_tensor',
                                out=ot[:, ep, :], in0=ps[:, A, 0:128], scalar=rl[:, ep, 0:1], in1=t2[:, ep, :],
                                op0=ALU.mult, op1=ALU.subtract), reads=[pb[A], eb], writes=[eb])
                            S.emit("dve", OP('scalar_tensor_tensor',
                                out=junk[:, ep, :], in0=ot[:, ep, :], scalar=1.0, in1=ot[:, ep, :],
                                op0=ALU.mult, op1=ALU.mult, accum_out=sst[:, ep, 0:1]), reads=[eb], writes=[eb])
                            S.emit("dve", OP('tensor_scalar', out=sst[:, ep, 1:2], in0=sst[:, ep, 0:1],
                                                                        scalar1=float(128.0 * SUBLN_EPS), scalar2=None,
                                                                        op0=ALU.add), reads=[eb], writes=[eb])
                            S.emit("pool", OP('tensor_tensor', out=sst[:, ep, 2:3], in0=sst[:, ep, 1:2],
                                                                          in1=mhalf[:, 0:1], op=ALU.pow),
                                   reads=[eb, cbuf], writes=[eb])
                            S.emit("dve", OP('scalar_tensor_tensor',
                                out=on[:, o_i, :], in0=ot[:, ep, :], scalar=sst[:, ep, 2:3], in1=gsub[:, :],
                                op0=ALU.mult, op1=ALU.mult), reads=[eb, cbuf], writes=[onb[o_i]])

                            def do_tr(qt=qt, o_i=o_i, qb=qb, h=h):
                                tbank = 0 if (qb % 2 == 0) else 2
                                psb = ps[:, tbank, :].bitcast(BF16)
                                jj = qt % 4
                                S.emit("pe", OP('transpose', psb[:, 0:128], on[:, o_i, :], identb[:, :]),
                                       reads=[onb[o_i], cbuf], writes=[pb[tbank]])
                                S.emit("dve", OP('tensor_copy', out=oT[:, h, 128 * qt:128 * (qt + 1)], in_=psb[:, 0:128]),
                                       reads=[pb[tbank]], writes=[oTb[h][qb]])

                            deferred.append(do_tr)
                flush_deferred()
            flush_deferred()
            S.barrier()
            AR.off = 0
            nt = NormTmp()
            pt = PostTmp()
            assert AR.off <= 8320
            proj_post(nt, pt, L * 4 + 1,
                      lambda kc, tb: oT[:, kc, 512 * tb:512 * (tb + 1)],
                      lambda tb: [oTb[kc][tb] for kc in range(KC)])

        setup()
        w_prefetch()
        phase_load_x()
        for L in layers:
            if 'dbg_g32' in parts:
                S.emit("dve", OP('tensor_copy', out=xT[:, 0, 0:64], in_=g32[:, :]), reads=[cbuf], writes=[xTb[0][0]])
                S.emit("dve", OP('tensor_copy', out=xT[:, 0, 64:328], in_=fcw[:, :]), reads=[cbuf], writes=[xTb[0][0]])
                S.emit("dve", OP('tensor_copy', out=xT[:, 0, 328:352], in_=cwt[:, :]), reads=[cbuf], writes=[xTb[0][0]])
                S.emit("dve", OP('tensor_copy', out=xT[:, 0, 352:360], in_=lamt[:, :]), reads=[cbuf], writes=[xTb[0][0]])
                S.emit("dve", OP('tensor_copy', out=xT[:, 0, 384:512], in_=gsub[:, :]), reads=[cbuf], writes=[xTb[0][0]])
            if 'dbg_prenorm' in parts:
                S.barrier()
                AR.reset()
                ntd = NormTmp()
                prenorm(ntd, L * 4 + 0, range(NB))
                for c in range(KC):
                    for tb in range(NB):
                        S.emit("dve", OP('tensor_copy', out=xT[:, c, 512 * tb:512 * (tb + 1)],
                                                                         in_=hT[:, c, 512 * tb:512 * (tb + 1)]),
                               reads=[hTb[c][tb]], writes=[xTb[c][tb]])
            if 'mix' in parts:
                S.barrier()
                if L == 0:
                    phase_attn(L)
                else:
                    phase_conv(L)
            if 'ffn' in parts:
                S.barrier()
                phase_ffn(L)
        S.barrier()
        phase_store_y()
        S.wait_all("sp", out_toks)

        sems = {}
        for k in list(S.cnt.keys()):
            nm = "s_" + "_".join(str(t) for t in (k if isinstance(k, tuple) else (k,)))
            sems[k] = E(nc.semaphore(nm))
        block = E(nc.Block())

        def replay(eng_name):
            def run(e):
                for (waits, fns, tok, inc) in S.ops[eng_name]:
                    for (k, v) in waits:
                        e.wait_ge(sems[k], v)
                    ins = None
                    for fn in fns:
                        ins = fn(e)
                    if tok is not None and ins is not None:
                        ins.then_inc(sems[tok[0]], inc)
            return run

        block.tensor(replay("pe"))
        block.scalar(replay("act"))
        block.vector(replay("dve"))
        block.gpsimd(replay("pool"))
        block.sync(replay("sp"))
    return nc


def make_consts():
    c = {}
    c["c_ident"] = np.eye(128, dtype=np.float32)
    c["c_ones"] = np.ones((128, 128), dtype=np.float32)
    p = np.arange(128, dtype=np.float32)
    ka = np.zeros((128, 128), np.float32)
    ka[0] = p
    ka[1] = 1.0
    ka[2] = 1.0
    c["c_kaug"] = ka
    qr = np.arange(512)
    qa = np.zeros((128, 512), np.float32)
    qa[0] = 1.0
    qa[1] = -(qr % 128)
    qa[2] = -128.0 * (qr // 128)
    c["c_qaug"] = qa
    k = np.arange(128)[:, None]
    q = np.arange(128)[None, :]
    allowed = (k // 64) <= (q // 64)
    c["c_maskd"] = np.where(allowed, -np.abs(q - k).astype(np.float32), np.float32(NEG_BIG)).astype(np.float32)
    bt = np.zeros((128, 128), np.float32)
    for h in range(H):
        slope = 2.0 ** (-(h + 1))
        for j in range(16):
            bt[:, h * 16 + j] = -128.0 * j * slope
    c["c_btab"] = bt
    return c


_NC_CACHE = {}


def kernel(x, norm_g, attn_w_qkv, attn_w_o, attn_lambda_q1, attn_lambda_k1, attn_lambda_q2,
           attn_lambda_k2, attn_subln_g, conv_w_in, conv_w, conv_w_out, ffn_w_up, ffn_conv_w, ffn_w_down):
    n = 8
    f = lambda a: np.ascontiguousarray(np.asarray(a, dtype=np.float32))
    shared = {
        "norm_g": f(norm_g), "attn_w_qkv": f(attn_w_qkv), "attn_w_o": f(attn_w_o),
        "attn_lambda_q1": f(attn_lambda_q1), "attn_lambda_k1": f(attn_lambda_k1),
        "attn_lambda_q2": f(attn_lambda_q2), "attn_lambda_k2": f(attn_lambda_k2),
        "attn_subln_g": f(attn_subln_g), "conv_w_in": f(conv_w_in), "conv_w": f(conv_w),
        "conv_w_out": f(conv_w_out), "ffn_w_up": f(ffn_w_up), "ffn_conv_w": f(ffn_conv_w),
        "ffn_w_down": f(ffn_w_down),
    }
    shared.update(make_consts())
    xs = f(x)
    nc = build_program((0, 1))
    in_maps = []
    for i in range(n):
        m = dict(shared)
        m["x"] = np.ascontiguousarray(xs[i])
        in_maps.append(m)
    res = run_bass_kernel_spmd(nc, in_maps, core_ids=list(range(n)))
    return np.stack([np.asarray(r["y"], dtype=np.float32) for r in res.results], axis=0)
```

```python
import math
from contextlib import ExitStack

import numpy as np
import concourse.bass as bass
import concourse.mybir as mybir
from concourse.bass_utils import run_bass_kernel_spmd

F32 = mybir.dt.float32
BF16 = mybir.dt.bfloat16
AF = mybir.ActivationFunctionType
ALU = mybir.AluOpType

S_LEN = 2048
D = 1024
NB = 4
KC = 8
DFF = 2816
NFI = 22
H = 8
NORM_EPS = 1e-6
SUBLN_EPS = 1e-5
NEG_BIG = -60000.0

ENGS = ("pe", "act", "dve", "pool", "sp")


def OP(name, *args, **kw):
    return lambda e: getattr(e, name)(*args, **kw)


class Buf:
    __slots__ = ("name", "w", "r")

    def __init__(self, name=""):
        self.name = name
        self.w = None
        self.r = {}


class Sched:
    def __init__(self):
        self.ops = {e: [] for e in ENGS}
        self.cnt = {}
        self.seen = {e: {} for e in ENGS}
        self.dma_keys = []

    def _waits(self, eng, reads, writes):
        deps = {}

        def add(tok):
            if tok is None:
                return
            k, v = tok
            if deps.get(k, 0) < v:
                deps[k] = v

        for b in reads:
            add(b.w)
        for b in writes:
            add(b.w)
            for k, v in b.r.items():
                add((k, v))
        waits = []
        for k, v in deps.items():
            if eng == "pe" and k == "pe":
                continue
            if self.seen[eng].get(k, 0) >= v:
                continue
            self.seen[eng][k] = v
            waits.append((k, v))
        return waits

    def _commit(self, tok, reads, writes):
        for b in reads:
            if b.r.get(tok[0], 0) < tok[1]:
                b.r[tok[0]] = tok[1]
        for b in writes:
            b.w = tok
            b.r = {}

    def emit(self, eng, fns, reads=(), writes=()):
        if not isinstance(fns, (list, tuple)):
            fns = [fns]
        waits = self._waits(eng, reads, writes)
        self.cnt[eng] = self.cnt.get(eng, 0) + 1
        tok = (eng, self.cnt[eng])
        self.ops[eng].append((waits, list(fns), tok, 1))
        self._commit(tok, reads, writes)
        return tok

    def dma(self, qeng, fn, key, reads=(), writes=(), nowait=False):
        waits = [] if nowait else self._waits(qeng, reads, writes)
        if key not in self.cnt:
            self.cnt[key] = 0
            self.dma_keys.append(key)
        self.cnt[key] += 16
        tok = (key, self.cnt[key])
        self.ops[qeng].append((waits, [fn], tok, 16))
        self._commit(tok, reads, writes)
        return tok

    def barrier(self):
        for e in ENGS:
            waits = []
            for k in ("pe", "act", "dve", "pool"):
                v = self.cnt.get(k, 0)
                if v == 0 or (e == k and e == "pe"):
                    continue
                if self.seen[e].get(k, 0) >= v:
                    continue
                self.seen[e][k] = v
                waits.append((k, v))
            if waits:
                self.ops[e].append((waits, [], None, 0))

    def wait_all(self, eng, toks):
        waits = []
        for k, v in toks:
            if self.seen[eng].get(k, 0) >= v:
                continue
            self.seen[eng][k] = v
            waits.append((k, v))
        if waits:
            self.ops[eng].append((waits, [], None, 0))


def build_program(layers=(0, 1), parts=('mix', 'ffn')):
    nc = bass.Bass("TRN2", target_bir_lowering=False)
    S = Sched()

    def din(name, shape):
        return nc.dram_tensor(name, list(shape), F32, kind="ExternalInput").ap()

    x_d = din("x", (S_LEN, D))
    norm_g_d = din("norm_g", (2, 4, D))
    wqkv_d = din("attn_w_qkv", (1, D, 3072))
    wo_d = din("attn_w_o", (1, D, D))
    lq1_d = din("attn_lambda_q1", (1, 64))
    lk1_d = din("attn_lambda_k1", (1, 64))
    lq2_d = din("attn_lambda_q2", (1, 64))
    lk2_d = din("attn_lambda_k2", (1, 64))
    subg_d = din("attn_subln_g", (1, 128))
    win_d = din("conv_w_in", (1, D, 3072))
    cw_d = din("conv_w", (1, 3, D))
    wout_d = din("conv_w_out", (1, D, D))
    wup_d = din("ffn_w_up", (2, D, 2 * DFF))
    fcw_d = din("ffn_conv_w", (2, 3, 2 * DFF))
    wdn_d = din("ffn_w_down", (2, DFF, D))
    ident_d = din("c_ident", (128, 128))
    ones_d = din("c_ones", (128, 128))
    kaug_d = din("c_kaug", (128, 128))
    qaug_d = din("c_qaug", (128, 512))
    maskd_d = din("c_maskd", (128, 128))
    btab_d = din("c_btab", (128, 128))
    y_d = nc.dram_tensor("y", [S_LEN, D], F32, kind="ExternalOutput").ap()

    es = ExitStack()
    with es:
        E = es.enter_context
        xT = E(nc.sbuf_tensor("xT", [128, KC, S_LEN], F32))
        hT = E(nc.sbuf_tensor("hT", [128, KC, S_LEN], BF16))
        NSLOT = 3
        wring = E(nc.sbuf_tensor("wring", [128, NSLOT, 4096], BF16))
        identf = E(nc.sbuf_tensor("identf", [128, 128], F32))
        identb = E(nc.sbuf_tensor("identb", [128, 128], BF16))
        onesb = E(nc.sbuf_tensor("onesb", [128, 128], BF16))
        kaug = E(nc.sbuf_tensor("kaug", [128, 128], BF16))
        qaug = E(nc.sbuf_tensor("qaug", [128, 512], BF16))
        maskd = E(nc.sbuf_tensor("maskd", [128, 128], BF16))
        btab = E(nc.sbuf_tensor("btab", [128, 128], F32))
        mhalf = E(nc.sbuf_tensor("mhalf", [128, 2], F32))
        epst = E(nc.sbuf_tensor("epst", [128, 2], F32))
        g32 = E(nc.sbuf_tensor("g32", [128, 64], F32))
        fcw = E(nc.sbuf_tensor("fcw", [128, 264], F32))
        cwt = E(nc.sbuf_tensor("cwt", [128, 24], F32))
        gsub = E(nc.sbuf_tensor("gsub", [128, 128], F32))
        lamt = E(nc.sbuf_tensor("lamt", [128, 8], F32))
        pstage = E(nc.sbuf_tensor("pstage", [128, 128], F32))
        lstage = E(nc.sbuf_tensor("lstage", [128, 4, 64], F32))
        ARENA_W = 20300
        arena = E(nc.sbuf_tensor("arena", [128, ARENA_W], F32))
        ps = E(nc.psum_tensor("ps", [128, 8, 512], F32))
        pb = [Buf("pb%d" % i) for i in range(8)]

        class Arena:
            def __init__(self):
                self.off = 0

            def reset(self):
                self.off = 0

            def f32(self, n):
                a = arena[:, self.off:self.off + n]
                self.off += n
                assert self.off <= ARENA_W, self.off
                return a

            def bf16(self, n):
                w = (n + 1) // 2
                a = arena[:, self.off:self.off + w].bitcast(BF16)
                self.off += w
                assert self.off <= ARENA_W, self.off
                return a

        AR = Arena()

        xTb = [[Buf("xT%d_%d" % (c, tb)) for tb in range(NB)] for c in range(KC)]
        hTb = [[Buf("hT%d_%d" % (c, tb)) for tb in range(NB)] for c in range(KC)]
        cbuf = Buf("consts")

        bank_rr = [0]

        def next_bank(lo=0, n=4):
            b = lo + bank_rr[0] % n
            bank_rr[0] += 1
            return b

        wslot_buf = [Buf("wslot%d" % i) for i in range(NSLOT)]
        wplan = []
        wstate = {"issued": 0, "taken": 0}

        def w_issue_upto(n):
            while wstate["issued"] < min(n, len(wplan)):
                i = wstate["issued"]
                slot = i % NSLOT
                for pi, (dst_fn, src) in enumerate(wplan[i]):
                    dst = dst_fn(slot)
                    S.dma("pool", OP('dma_start', out=dst, in_=src),
                          ("w", slot), writes=[wslot_buf[slot]], nowait=(pi > 0))
                wstate["issued"] += 1

        def w_next():
            i = wstate["taken"]
            w_issue_upto(i + 1)
            wstate["taken"] += 1
            return i % NSLOT, wslot_buf[i % NSLOT]

        def w_prefetch():
            w_issue_upto(wstate["taken"] + NSLOT)

        def wview(slot, dims):
            n = int(np.prod(dims))
            a = wring[:, slot, 0:n]
            if len(dims) == 2:
                return a.rearrange("p (a b) -> p a b", a=dims[0])
            if len(dims) == 3:
                return a.rearrange("p (a b c) -> p a b c", a=dims[0], b=dims[1])
            return a

        def plan_weights():
            def rows(wd):
                return wd.rearrange("(kc p) n -> p kc n", p=128)

            for L in layers:
                if 'mix' not in parts:
                    pass
                elif L == 0:
                    w = rows(wqkv_d[0])
                    for g in range(2):
                        wplan.append([(lambda s: wview(s, (KC, 512)), w[:, :, 2048 + 512 * g:2048 + 512 * (g + 1)])])
                    for h in range(H):
                        ld = []
                        for j, base in enumerate((0, 512, 1024, 1536)):
                            ld.append((lambda s, j=j: wview(s, (KC, 256))[:, :, 64 * j:64 * (j + 1)],
                                       w[:, :, base + 64 * h:base + 64 * (h + 1)]))
                        wplan.append(ld)
                    w = rows(wo_d[0])
                    for g in range(2):
                        wplan.append([(lambda s: wview(s, (KC, 512)), w[:, :, 512 * g:512 * (g + 1)])])
                else:
                    w = rows(win_d[0])
                    for j in range(KC):
                        ld = []
                        for sec in range(3):
                            ld.append((lambda s, sec=sec: wview(s, (KC, 3, 128))[:, :, sec, :],
                                       w[:, :, sec * 1024 + 128 * j:sec * 1024 + 128 * (j + 1)]))
                        wplan.append(ld)
                    w = rows(wout_d[0])
                    for g in range(2):
                        wplan.append([(lambda s: wview(s, (KC, 512)), w[:, :, 512 * g:512 * (g + 1)])])
                wu = rows(wup_d[L])
                wd = wdn_d[L].rearrange("(fi p) n -> p fi n", p=128)
                for tb in range(NB if 'ffn' in parts else 0):
                    for fp in range(NFI // 2):
                        ld = []
                        for gv in range(2):
                            ld.append((lambda s, gv=gv: wview(s, (2, KC, 256))[:, gv, :, :],
                                       wu[:, :, gv * DFF + 256 * fp:gv * DFF + 256 * (fp + 1)]))
                        wplan.append(ld)
                    for c in range(KC):
                        wplan.append([(lambda s: wview(s, (NFI, 128)), wd[:, :, 128 * c:128 * (c + 1)])])

        plan_weights()

        def setup():
            def cload(q, dst, src, key):
                S.dma(q, OP('dma_start', out=dst, in_=src), key, writes=[cbuf])

            cload("sp", identf[:, :], ident_d[:, :], "c0")
            cload("sp", btab[:, :], btab_d[:, :], "c0")
            cload("pool", identb[:, :], ident_d[:, :], "c1")
            cload("pool", onesb[:, :], ones_d[:, :], "c1")
            cload("pool", kaug[:, :], kaug_d[:, :], "c1")
            cload("pool", qaug[:, :], qaug_d[:, :], "c1")
            cload("pool", maskd[:, :], maskd_d[:, :], "c1")
            S.emit("pool", OP('memset', mhalf[:, :], -0.5), writes=[cbuf])
            S.emit("pool", OP('memset', epst[:, :], float(D * NORM_EPS)), writes=[cbuf])
            pst = Buf("pstage")

            def tr_param(src2d, nrows, dst, scale):
                S.dma("sp", (OP('dma_start', out=pstage[0:nrows, :], in_=src2d)), "c2", writes=[pst])
                bank = next_bank()
                S.emit("pe", OP('transpose', ps[:, bank, 0:nrows], pstage[0:nrows, :], identf[0:nrows, 0:nrows]),
                       reads=[pst, cbuf], writes=[pb[bank]])
                S.emit("dve", OP('tensor_scalar', out=dst, in0=ps[:, bank, 0:nrows], scalar1=float(scale),
                                                       scalar2=None, op0=ALU.mult),
                       reads=[pb[bank]], writes=[cbuf])

            tr_param(norm_g_d.rearrange("l i (c p) -> (l i c) p", p=128), 64, g32[:, :], 32.0)
            f2 = fcw_d.rearrange("l j (c p) -> (l j c) p", p=128)
            for i in range(3):
                tr_param(f2[88 * i:88 * (i + 1), :], 88, fcw[:, 88 * i:88 * (i + 1)], 1.0)
            tr_param(cw_d.rearrange("l j (c p) -> (l j c) p", p=128), 24, cwt[:, :], 1.0)
            lam_init = 0.8 - 0.6 * math.exp(-0.3 * 0)
            S.dma("sp", (OP('dma_start', out=gsub[:, :], in_=subg_d[0:1, :].broadcast_to([128, 128]))),
                  "c3", writes=[cbuf])
            S.emit("dve", OP('tensor_scalar', out=gsub[:, :], in0=gsub[:, :],
                                                   scalar1=float((1.0 - lam_init) * math.sqrt(128.0)),
                                                   scalar2=None, op0=ALU.mult), reads=[cbuf], writes=[cbuf])
            for i, ldd in enumerate((lq1_d, lk1_d, lq2_d, lk2_d)):
                S.dma("sp", (OP('dma_start', out=lstage[:, i, :],
                                                                 in_=ldd[0:1, :].broadcast_to([128, 64]))),
                      "c3", writes=[cbuf])
            S.emit("dve", OP('tensor_tensor', out=lstage[:, 0, :], in0=lstage[:, 0, :], in1=lstage[:, 1, :],
                                                   op=ALU.mult), reads=[cbuf], writes=[cbuf])
            S.emit("dve", OP('tensor_tensor', out=lstage[:, 2, :], in0=lstage[:, 2, :], in1=lstage[:, 3, :],
                                                   op=ALU.mult), reads=[cbuf], writes=[cbuf])
            S.emit("dve", OP('reduce_sum', out=lamt[:, 0:1], in_=lstage[:, 0, :], axis=mybir.AxisListType.X),
                   reads=[cbuf], writes=[cbuf])
            S.emit("dve", OP('reduce_sum', out=lamt[:, 1:2], in_=lstage[:, 2, :], axis=mybir.AxisListType.X),
                   reads=[cbuf], writes=[cbuf])
            S.emit("act", OP('activation', out=lamt[:, 2:4], in_=lamt[:, 0:2], func=AF.Exp),
                   reads=[cbuf], writes=[cbuf])
            S.emit("dve", OP('tensor_tensor', out=lamt[:, 4:5], in0=lamt[:, 2:3], in1=lamt[:, 3:4],
                                                   op=ALU.subtract), reads=[cbuf], writes=[cbuf])
            S.emit("dve", OP('tensor_scalar', out=lamt[:, 5:6], in0=lamt[:, 4:5], scalar1=float(lam_init),
                                                   scalar2=None, op0=ALU.add), reads=[cbuf], writes=[cbuf])

        def phase_load_x():
            AR.reset()
            xs = AR.f32(8 * 1024).rearrange("p (s n) -> p s n", s=8)
            xsb = [Buf("xs%d" % i) for i in range(8)]

            def load(t):
                sl = t % 8
                S.dma("sp", (OP('dma_start', out=xs[:, sl, :], in_=x_d[128 * t:128 * (t + 1), :])),
                      ("xs", sl), writes=[xsb[sl]])

            for t in range(8):
                load(t)
            for tb in range(NB):
                for c in range(KC):
                    bank = next_bank()
                    fns = []
                    for j in range(4):
                        sl = (4 * tb + j) % 8
                        fns.append(OP('transpose',
                            ps[:, bank, 128 * j:128 * (j + 1)], xs[:, sl, 128 * c:128 * (c + 1)], identf[:, :]))
                    S.emit("pe", fns, reads=[xsb[(4 * tb + j) % 8] for j in range(4)] + [cbuf], writes=[pb[bank]])
                    dst = xT[:, c, 512 * tb:512 * (tb + 1)]
                    if c % 2 == 0:
                        S.emit("act", OP('activation', out=dst, in_=ps[:, bank, :], func=AF.Copy),
                               reads=[pb[bank]], writes=[xTb[c][tb]])
                    else:
                        S.emit("dve", OP('tensor_copy', out=dst, in_=ps[:, bank, :]),
                               reads=[pb[bank]], writes=[xTb[c][tb]])
                if tb + 2 < NB:
                    for j in range(4):
                        load(4 * (tb + 2) + j)

        out_toks = []

        def phase_store_y():
            AR.reset()
            ys = AR.f32(4 * 1024).rearrange("p (s n) -> p s n", s=4)
            ysb = [Buf("ys%d" % i) for i in range(4)]
            for t in range(16):
                tb = t // 4
                sl = t % 4
                for half in range(2):
                    bank = next_bank()
                    fns = []
                    for j in range(4):
                        c = 4 * half + j
                        fns.append(OP('transpose',
                            ps[:, bank, 128 * j:128 * (j + 1)], xT[:, c, 128 * t:128 * (t + 1)], identf[:, :]))
                    S.emit("pe", fns, reads=[xTb[4 * half + j][tb] for j in range(4)] + [cbuf], writes=[pb[bank]])
                    dst = ys[:, sl, 512 * half:512 * (half + 1)]
                    if half == 0:
                        S.emit("act", OP('activation', out=dst, in_=ps[:, bank, :], func=AF.Copy),
                               reads=[pb[bank]], writes=[ysb[sl]])
                    else:
                        S.emit("dve", OP('tensor_copy', out=dst, in_=ps[:, bank, :]),
                               reads=[pb[bank]], writes=[ysb[sl]])
                tok = S.dma("sp", (OP('dma_start', out=y_d[128 * t:128 * (t + 1), :], in_=ys[:, sl, :])),
                            ("ys", sl), reads=[ysb[sl]])
                out_toks.append(tok)

        class NormTmp:
            def __init__(self, t1n=2):
                self.sq = AR.bf16(4 * 512).rearrange("p (s n) -> p s n", s=4)
                self.sqb = [Buf("sq%d" % i) for i in range(4)]
                self.t1n = t1n
                self.t1 = AR.f32(t1n * 512).rearrange("p (s n) -> p s n", s=t1n)
                self.t1b = [Buf("t1_%d" % i) for i in range(t1n)]
                self.rstd = AR.f32(2 * 512).rearrange("p (s n) -> p s n", s=2)
                self.rstdb = [Buf("rstd%d" % i) for i in range(2)]
                self.k = 0
                self.r = 0

            def next_sq(self):
                i = self.k % 4
                self.k += 1
                return i

            def next_r(self):
                i = self.r % 2
                self.r += 1
                return i

            def t1i(self, r):
                return r % self.t1n

        def emit_rstd(nt, ssbank, eps_total):
            r = nt.next_r()
            ti = nt.t1i(r)
            S.emit("act", OP('activation', out=nt.t1[:, ti, :], in_=ps[:, ssbank, :], func=AF.Ln, bias=epst[:, 0:1]),
                   reads=[pb[ssbank], cbuf], writes=[nt.t1b[ti]])
            S.emit("act", OP('activation', out=nt.rstd[:, r, :], in_=nt.t1[:, ti, :], func=AF.Exp, scale=-0.5),
                   reads=[nt.t1b[ti]], writes=[nt.rstdb[r]])
            return r

        def prenorm(nt, gidx, tbs):
            for tb in tbs:
                blk = slice(512 * tb, 512 * (tb + 1))
                ssbank = 7
                for c in range(KC):
                    i = nt.next_sq()
                    S.emit("act", OP('activation', out=nt.sq[:, i, :], in_=xT[:, c, blk], func=AF.Square),
                           reads=[xTb[c][tb]], writes=[nt.sqb[i]])
                    S.emit("pe", OP('matmul', ps[:, ssbank, :], onesb[:, :], nt.sq[:, i, :],
                                                             start=(c == 0), stop=(c == KC - 1)),
                           reads=[nt.sqb[i], cbuf], writes=[pb[ssbank]])
                r = emit_rstd(nt, ssbank, D * NORM_EPS)
                for c in range(KC):
                    S.emit("dve", OP('scalar_tensor_tensor',
                        out=hT[:, c, blk], in0=xT[:, c, blk], scalar=g32[:, gidx * 8 + c:gidx * 8 + c + 1],
                        in1=nt.rstd[:, r, :], op0=ALU.mult, op1=ALU.mult),
                           reads=[xTb[c][tb], nt.rstdb[r], cbuf], writes=[hTb[c][tb]])

        class PostTmp:
            def __init__(self):
                self.msb = AR.f32(KC * 512).rearrange("p (c n) -> p c n", c=KC)
                self.msbb = [Buf("msb%d" % i) for i in range(KC)]
                self.tp = AR.f32(2 * 512).rearrange("p (s n) -> p s n", s=2)
                self.tpb = [Buf("tp%d" % i) for i in range(2)]
                self.k = 0

        def post_block(nt, pt, gidx, tb, mm_group, banks=(0, 4)):
            blk = slice(512 * tb, 512 * (tb + 1))
            ssbank = 7
            pending = None

            def ss_mm(c, i):
                S.emit("pe", OP('matmul', ps[:, ssbank, :], onesb[:, :], nt.sq[:, i, :],
                                                start=(c == 0), stop=(c == KC - 1)),
                       reads=[nt.sqb[i], cbuf], writes=[pb[ssbank]])

            for c in range(KC):
                bank = next_bank(*banks)
                mm_group(c, bank)
                if pending is not None:
                    ss_mm(*pending)
                S.emit("act", OP('activation', out=pt.msb[:, c, :], in_=ps[:, bank, :], func=AF.Copy),
                       reads=[pb[bank]], writes=[pt.msbb[c]])
                i = nt.next_sq()
                S.emit("act", OP('activation', out=nt.sq[:, i, :], in_=pt.msb[:, c, :], func=AF.Square),
                       reads=[pt.msbb[c]], writes=[nt.sqb[i]])
                pending = (c, i)
            ss_mm(*pending)
            r = emit_rstd(nt, ssbank, D * NORM_EPS)
            for c in range(KC):
                j = pt.k % 2
                pt.k += 1
                S.emit("dve", OP('scalar_tensor_tensor',
                    out=pt.tp[:, j, :], in0=pt.msb[:, c, :], scalar=g32[:, gidx * 8 + c:gidx * 8 + c + 1],
                    in1=nt.rstd[:, r, :], op0=ALU.mult, op1=ALU.mult),
                       reads=[pt.msbb[c], nt.rstdb[r], cbuf], writes=[pt.tpb[j]])
                S.emit("pool", OP('tensor_tensor', out=xT[:, c, blk], in0=xT[:, c, blk], in1=pt.tp[:, j, :],
                                                                  op=ALU.add),
                       reads=[pt.tpb[j], xTb[c][tb]], writes=[xTb[c][tb]])

        def proj_post(nt, pt, gidx, in_ap, in_bufs):
            s0, b0 = w_next()
            s1, b1 = w_next()
            wt = [wview(s0, (KC, 512)), wview(s1, (KC, 512))]
            wb = [b0, b1]
            for tb in range(NB):
                def grp(c, bank, tb=tb):
                    w = wt[c // 4]
                    col = 128 * (c % 4)
                    fns = [(OP('matmul', ps[:, bank, :], w[:, kc, col:col + 128], in_ap(kc, tb),
                                                      start=(kc == 0), stop=(kc == KC - 1))) for kc in range(KC)]
                    S.emit("pe", fns, reads=[wb[c // 4]] + in_bufs(tb), writes=[pb[bank]])

                post_block(nt, pt, gidx, tb, grp)
            w_prefetch()

        def phase_ffn(L):
            AR.reset()
            nt = NormTmp(t1n=1)
            pt = PostTmp()
            aT = AR.bf16(NFI * 512).rearrange("p (f n) -> p f n", f=NFI)
            aTb = [Buf("aT%d" % i) for i in range(NFI)]
            NU = 6
            ug = AR.f32(NU * 516).rearrange("p (s n) -> p s n", s=NU)
            ugb = [Buf("ug%d" % i) for i in range(NU)]
            tcv = AR.f32(NU * 512).rearrange("p (s n) -> p s n", s=NU)
            tcb = [Buf("tc%d" % i) for i in range(NU)]
            NSG = 2
            sg = AR.bf16(NSG * 512).rearrange("p (s n) -> p s n", s=NSG)
            sgb = [Buf("sg%d" % i) for i in range(NSG)]
            halo = AR.f32(44 * 2).rearrange("p (c n) -> p c n", c=44)
            halob = [Buf("halo%d" % i) for i in range(44)]
            uk = [0]
            sk = [0]
            pendB = []
            pendC = []

            def stageB(item):
                fi, ugate, uval = item
                si = sk[0] % NSG
                sk[0] += 1
                S.emit("act", OP('activation', out=sg[:, si, :], in_=tcv[:, ugate, :], func=AF.Silu),
                       reads=[tcb[ugate]], writes=[sgb[si]])
                return (fi, si, uval)

            def stageC(item):
                fi, si, uval = item
                S.emit("dve", OP('tensor_tensor', out=aT[:, fi, :], in0=sg[:, si, :], in1=tcv[:, uval, :], op=ALU.mult),
                       reads=[sgb[si], tcb[uval]], writes=[aTb[fi]])

            prenorm(nt, L * 4 + 2, [0])
            for tb in range(NB):
                blk = slice(512 * tb, 512 * (tb + 1))
                for fp in range(NFI // 2):
                    slot, wbuf = w_next()
                    wt = wview(slot, (2, KC, 256))
                    for f2 in range(2):
                        fi = 2 * fp + f2
                        tcs = []
                        for gv in range(2):
                            ch = gv * NFI + fi
                            bank = next_bank(0, 7)
                            fns = [OP('matmul', ps[:, bank, :], wt[:, gv, kc, 128 * f2:128 * (f2 + 1)], hT[:, kc, blk],
                                      start=(kc == 0), stop=(kc == KC - 1)) for kc in range(KC)]
                            S.emit("pe", fns, reads=[wbuf] + [hTb[kc][tb] for kc in range(KC)], writes=[pb[bank]])
                            u = uk[0] % NU
                            uk[0] += 1
                            if tb == 0:
                                S.emit("pool", OP('memset', ug[:, u, 0:2], 0.0), writes=[ugb[u]])
                            else:
                                S.emit("pool", OP('tensor_copy', out=ug[:, u, 0:2], in_=halo[:, ch, :]),
                                       reads=[halob[ch]], writes=[ugb[u]])
                            S.emit("act", OP('activation', out=ug[:, u, 2:514], in_=ps[:, bank, :], func=AF.Copy),
                                   reads=[pb[bank]], writes=[ugb[u]])
                            wi = lambda j, ch=ch: fcw[:, (L * 3 + j) * 44 + ch:(L * 3 + j) * 44 + ch + 1]
                            S.emit("act", OP('activation', out=tcv[:, u, :], in_=ps[:, bank, :], func=AF.Copy, scale=wi(2)),
                                   reads=[pb[bank], cbuf], writes=[tcb[u]])
                            if tb < NB - 1:
                                S.emit("pool", OP('tensor_copy', out=halo[:, ch, :], in_=ug[:, u, 512:514]),
                                       reads=[ugb[u]], writes=[halob[ch]])
                            S.emit("dve", OP('scalar_tensor_tensor', out=tcv[:, u, :], in0=ug[:, u, 1:513], scalar=wi(1),
                                             in1=tcv[:, u, :], op0=ALU.mult, op1=ALU.add),
                                   reads=[ugb[u], tcb[u], cbuf], writes=[tcb[u]])
                            S.emit("dve", OP('scalar_tensor_tensor', out=tcv[:, u, :], in0=ug[:, u, 0:512], scalar=wi(0),
                                             in1=tcv[:, u, :], op0=ALU.mult, op1=ALU.add),
                                   reads=[ugb[u], tcb[u], cbuf], writes=[tcb[u]])
                            tcs.append(u)
                        pendB.append((fi, tcs[0], tcs[1]))
                        if len(pendB) > 1:
                            pendC.append(stageB(pendB.pop(0)))
                        if len(pendC) > 1:
                            stageC(pendC.pop(0))
                    w_prefetch()
                while pendB:
                    pendC.append(stageB(pendB.pop(0)))
                while pendC:
                    stageC(pendC.pop(0))
                if tb + 1 < NB:
                    prenorm(nt, L * 4 + 2, [tb + 1])

                def grp(c, bank):
                    slot, wbuf = w_next()
                    wd = wview(slot, (NFI, 128))
                    fns = [OP('matmul', ps[:, bank, :], wd[:, fi, :], aT[:, fi, :],
                              start=(fi == 0), stop=(fi == NFI - 1)) for fi in range(NFI)]
                    S.emit("pe", fns, reads=[wbuf] + aTb, writes=[pb[bank]])
                    w_prefetch()

                post_block(nt, pt, L * 4 + 3, tb, grp, banks=(0, 7))

        def phase_conv(L):
            AR.reset()
            nt = NormTmp()
            pt = PostTmp()
            gT = AR.bf16(KC * S_LEN).rearrange("p (c n) -> p c n", c=KC)
            gTb = [[Buf("gT%d_%d" % (c, tb)) for tb in range(NB)] for c in range(KC)]
            uc = AR.f32(2 * 512).rearrange("p (s n) -> p s n", s=2)
            ucb = [Buf("uc%d" % i) for i in range(2)]
            z = AR.f32(2 * 516).rearrange("p (s n) -> p s n", s=2)
            zb = [Buf("z%d" % i) for i in range(2)]
            tcv = AR.f32(2 * 512).rearrange("p (s n) -> p s n", s=2)
            tcb = [Buf("tcc%d" % i) for i in range(2)]
            prenorm(nt, L * 4 + 0, range(NB))
            k = 0
            for j in range(KC):
                slot, wbuf = w_next()
                wt = wview(slot, (KC, 3, 128))
                for tb in range(NB):
                    blk = slice(512 * tb, 512 * (tb + 1))
                    banks = []
                    for sec in range(3):
                        bank = next_bank(0, 6)
                        fns = [(OP('matmul',
                            ps[:, bank, :], wt[:, kc, sec, :], hT[:, kc, blk],
                            start=(kc == 0), stop=(kc == KC - 1))) for kc in range(KC)]
                        S.emit("pe", fns, reads=[wbuf] + [hTb[kc][tb] for kc in range(KC)], writes=[pb[bank]])
                        banks.append(bank)
                    bb, bc, bh = banks
                    r = k % 2
                    k += 1
                    S.emit("act", OP('activation', out=uc[:, r, :], in_=ps[:, bc, :], func=AF.Copy),
                           reads=[pb[bc]], writes=[ucb[r]])
                    if tb == 0:
                        S.emit("pool", OP('memset', z[:, r, 0:2], 0.0), writes=[zb[r]])
                    else:
                        S.emit("pool", OP('tensor_copy', out=z[:, r, 0:2], in_=z[:, 1 - r, 512:514]),
                               reads=[zb[1 - r]], writes=[zb[r]])
                    S.emit("dve", OP('tensor_tensor', out=z[:, r, 2:514], in0=uc[:, r, :], in1=ps[:, bh, :],
                                                                      op=ALU.mult),
                           reads=[ucb[r], pb[bh]], writes=[zb[r]])
                    wi = lambda jj, j=j: cwt[:, jj * 8 + j:jj * 8 + j + 1]
                    S.emit("dve", OP('tensor_scalar', out=tcv[:, r, :], in0=z[:, r, 2:514], scalar1=wi(2),
                                                                     scalar2=None, op0=ALU.mult),
                           reads=[zb[r], cbuf], writes=[tcb[r]])
                    S.emit("dve", OP('scalar_tensor_tensor',
                        out=tcv[:, r, :], in0=z[:, r, 1:513], scalar=wi(1), in1=tcv[:, r, :], op0=ALU.mult, op1=ALU.add),
                           reads=[zb[r], tcb[r], cbuf], writes=[tcb[r]])
                    S.emit("dve", OP('scalar_tensor_tensor',
                        out=tcv[:, r, :], in0=z[:, r, 0:512], scalar=wi(0), in1=tcv[:, r, :], op0=ALU.mult, op1=ALU.add),
                           reads=[zb[r], tcb[r], cbuf], writes=[tcb[r]])
                    S.emit("dve", OP('tensor_tensor',
                        out=gT[:, j, blk], in0=tcv[:, r, :], in1=ps[:, bb, :], op=ALU.mult),
                           reads=[tcb[r], pb[bb]], writes=[gTb[j][tb]])
                w_prefetch()
            proj_post(nt, pt, L * 4 + 1,
                      lambda kc, tb: gT[:, kc, 512 * tb:512 * (tb + 1)],
                      lambda tb: [gTb[kc][tb] for kc in range(KC)])

        def phase_attn(L):
            AR.reset()
            vaug = AR.bf16(H * 16 * 130).rearrange("p (h t e) -> p h t e", h=H, t=16)
            vb = [[Buf("v%d_%d" % (g, t)) for t in range(16)] for g in range(2)]
            oT = AR.bf16(H * S_LEN).rearrange("p (h n) -> p h n", h=H)
            oTb = [[Buf("oT%d_%d" % (h, tb)) for tb in range(NB)] for h in range(H)]
            QQ = AR.bf16(S_LEN)
            KK = AR.bf16(S_LEN)
            QQb = [Buf("QQ%d" % tb) for tb in range(NB)]
            KKb = [Buf("KK%d" % tb) for tb in range(NB)]
            Pt = AR.bf16(2 * 2 * 512).rearrange("p (s m n) -> p s m n", s=2, m=2)
            Ptb = [Buf("Pt%d" % i) for i in range(2)]
            NEP = 2
            rl = AR.f32(NEP * 4).rearrange("p (s n) -> p s n", s=NEP)
            t2 = AR.f32(NEP * 128).rearrange("p (s n) -> p s n", s=NEP)
            ot = AR.f32(NEP * 128).rearrange("p (s n) -> p s n", s=NEP)
            junk = t2
            sst = AR.f32(NEP * 4).rearrange("p (s n) -> p s n", s=NEP)
            epb = [Buf("ep%d" % i) for i in range(NEP)]
            NON = 2
            on = AR.bf16(NON * 128).rearrange("p (s n) -> p s n", s=NON)
            onb = [Buf("on%d" % i) for i in range(NON)]
            AR.off = 8320
            nt0 = NormTmp()
            prenorm(nt0, L * 4 + 0, range(NB))
            S.barrier()
            S.emit("pool", OP('memset', vaug[:, :, :, 128:129], 1.0), writes=[b for g in vb for b in g])
            ev = 0
            for g in range(2):
                slot, wbuf = w_next()
                wv = wview(slot, (KC, 512))
                for t in range(16):
                    bank = next_bank()
                    fns = [(OP('matmul',
                        ps[:, bank, :], hT[:, kc, 128 * t:128 * (t + 1)], wv[:, kc, :],
                        start=(kc == 0), stop=(kc == KC - 1))) for kc in range(KC)]
                    S.emit("pe", fns, reads=[wbuf] + [hTb[kc][t // 4] for kc in range(KC)], writes=[pb[bank]])
                    dst = vaug[:, 4 * g:4 * g + 4, t, 0:128]
                    src = ps[:, bank, :].rearrange("p (h e) -> p h e", h=4)
                    if ev % 2 == 0:
                        S.emit("act", OP('activation', out=dst, in_=src, func=AF.Copy),
                               reads=[pb[bank]], writes=[vb[g][t]])
                    else:
                        S.emit("dve", OP('tensor_copy', out=dst, in_=src),
                               reads=[pb[bank]], writes=[vb[g][t]])
                    ev += 1
                w_prefetch()
            pk = [0]
            epk = [0]
            onk = [0]
            deferred = []

            def flush_deferred():
                while deferred:
                    deferred.pop(0)()

            for h in range(H):
                slope = 2.0 ** (-(h + 1))
                qscale = 1.0 / (8.0 * slope)
                slot, wbuf = w_next()
                wqk = wview(slot, (KC, 256))
                for tb in range(NB):
                    blk = slice(512 * tb, 512 * (tb + 1))
                    for which in range(2):
                        bank = next_bank()
                        fns = [(OP('matmul',
                            ps[:, bank, :], wqk[:, kc, 128 * which:128 * (which + 1)], hT[:, kc, blk],
                            start=(kc == 0), stop=(kc == KC - 1))) for kc in range(KC)]
                        S.emit("pe", fns, reads=[wbuf] + [hTb[kc][tb] for kc in range(KC)], writes=[pb[bank]])
                        if which == 0:
                            S.emit("act", OP('activation', out=QQ[:, blk], in_=ps[:, bank, :],
                                                                                  func=AF.Copy, scale=float(qscale)),
                                   reads=[pb[bank]], writes=[QQb[tb]])
                        else:
                            S.emit("dve", OP('tensor_copy', out=KK[:, blk], in_=ps[:, bank, :]),
                                   reads=[pb[bank]], writes=[KKb[tb]])
                w_prefetch()
                steps = [(qb, kt) for qb in range(NB) for kt in range(4 * qb + 4)]

                def step_geom(qb, kt):
                    lo = max(128 * kt, 512 * qb)
                    hi = 512 * (qb + 1)
                    return lo, hi, hi - lo, kt >= 4 * qb

                def emit_scores(i):
                    qb, kt = steps[i]
                    lo, hi, N, diag = step_geom(qb, kt)
                    par = (pk[0] + i) % 2
                    sb = 2 * par
                    fns = []
                    for m in range(2):
                        rs = slice(64 * m, 64 * (m + 1))
                        fns.append(OP('matmul', ps[:, sb + m, 0:N], KK[rs, 128 * kt:128 * (kt + 1)], QQ[rs, lo:hi],
                                      start=True, stop=False))
                    off0 = 128 if diag else 0
                    for m in range(2):
                        if diag:
                            fns.append(OP('matmul', ps[:, sb + m, 0:128], identb[:, :], maskd[:, :],
                                          start=False, stop=(N == 128)))
                        if N > off0:
                            fns.append(OP('matmul', ps[:, sb + m, off0:N], kaug[:, :], qaug[:, off0:N],
                                          start=False, stop=True))
                    S.emit("pe", fns, reads=[KKb[kt // 4], QQb[qb], cbuf], writes=[pb[sb], pb[sb + 1]])
                    jb = (lo - 128 * kt) // 128
                    bias_ap = btab[:, h * 16 + jb:h * 16 + jb + 1]
                    S.emit("act", OP('activation', out=Pt[:, par, :, 0:N], in_=ps[:, sb:sb + 2, 0:N], func=AF.Exp,
                                     scale=float(slope), bias=bias_ap),
                           reads=[pb[sb], pb[sb + 1], cbuf], writes=[Ptb[par]])

                def emit_av(i):
                    qb, kt = steps[i]
                    lo, hi, N, diag = step_geom(qb, kt)
                    par = (pk[0] + i) % 2
                    fns = []
                    wr = []
                    for j in range(N // 128):
                        qt = lo // 128 + j
                        A = 4 + qt % 4
                        for m in range(2):
                            fns.append(OP('matmul', ps[:, A, 130 * m:130 * m + 129], Pt[:, par, m, 128 * j:128 * (j + 1)],
                                          vaug[:, h, kt, 0:129], start=(kt == 0 and m == 0), stop=(kt == qt),
                                          skip_group_check=True))
                        wr.append(pb[A])
                    S.emit("pe", fns, reads=[Ptb[par], vb[h // 4][kt]], writes=wr)
                    if kt >= 2:
                        flush_deferred()
                    if diag:
                        emit_epilogue(qb, kt)

                def emit_epilogue(qb, qt):
                    A = 4 + qt % 4
                    ep = epk[0] % NEP
                    epk[0] += 1
                    o_i = onk[0] % NON
                    onk[0] += 1
                    eb = epb[ep]
                    S.emit("dve", OP('reciprocal', out=rl[:, ep, 0:2], in_=ps[:, A, 128:259:130]),
                           reads=[pb[A]], writes=[eb])
                    S.emit("dve", OP('tensor_tensor', out=rl[:, ep, 2:3], in0=rl[:, ep, 1:2], in1=lamt[:, 5:6], op=ALU.mult),
                           reads=[eb, cbuf], writes=[eb])
                    S.emit("dve", OP('tensor_scalar', out=t2[:, ep, :], in0=ps[:, A, 130:258], scalar1=rl[:, ep, 2:3],
                                     scalar2=None, op0=ALU.mult), reads=[pb[A], eb], writes=[eb])
                    S.emit("dve", OP('scalar_tensor_tensor', out=ot[:, ep, :], in0=ps[:, A, 0:128], scalar=rl[:, ep, 0:1],
                                     in1=t2[:, ep, :], op0=ALU.mult, op1=ALU.subtract), reads=[pb[A], eb], writes=[eb])
                    S.emit("dve", OP('scalar_tensor_tensor', out=junk[:, ep, :], in0=ot[:, ep, :], scalar=1.0,
                                     in1=ot[:, ep, :], op0=ALU.mult, op1=ALU.mult, accum_out=sst[:, ep, 0:1]),
                           reads=[eb], writes=[eb])
                    S.emit("dve", OP('tensor_scalar', out=sst[:, ep, 1:2], in0=sst[:, ep, 0:1],
                                     scalar1=float(128.0 * SUBLN_EPS), scalar2=None, op0=ALU.add), reads=[eb], writes=[eb])
                    S.emit("pool", OP('tensor_tensor', out=sst[:, ep, 2:3], in0=sst[:, ep, 1:2], in1=mhalf[:, 0:1], op=ALU.pow),
                           reads=[eb, cbuf], writes=[eb])
                    S.emit("dve", OP('scalar_tensor_tensor', out=on[:, o_i, :], in0=ot[:, ep, :], scalar=sst[:, ep, 2:3],
                                     in1=gsub[:, :], op0=ALU.mult, op1=ALU.mult), reads=[eb, cbuf], writes=[onb[o_i]])

                    def do_tr(qt=qt, o_i=o_i, qb=qb, h=h):
                        tbank = 0 if (qb % 2 == 0) else 2
                        psb = ps[:, tbank, :].bitcast(BF16)
                        S.emit("pe", OP('transpose', psb[:, 0:128], on[:, o_i, :], identb[:, :]),
                               reads=[onb[o_i], cbuf], writes=[pb[tbank]])
                        S.emit("dve", OP('tensor_copy', out=oT[:, h, 128 * qt:128 * (qt + 1)], in_=psb[:, 0:128]),
                               reads=[pb[tbank]], writes=[oTb[h][qb]])

                    deferred.append(do_tr)

                emit_scores(0)
                for i in range(len(steps)):
                    if i + 1 < len(steps):
                        emit_scores(i + 1)
                    emit_av(i)
                pk[0] += len(steps)
                flush_deferred()
            flush_deferred()
            S.barrier()
            AR.off = 0
            nt = NormTmp()
            pt = PostTmp()
            assert AR.off <= 8320
            proj_post(nt, pt, L * 4 + 1,
                      lambda kc, tb: oT[:, kc, 512 * tb:512 * (tb + 1)],
                      lambda tb: [oTb[kc][tb] for kc in range(KC)])

        setup()
        w_prefetch()
        phase_load_x()
        for L in layers:
            if 'dbg_g32' in parts:
                S.emit("dve", OP('tensor_copy', out=xT[:, 0, 0:64], in_=g32[:, :]), reads=[cbuf], writes=[xTb[0][0]])
                S.emit("dve", OP('tensor_copy', out=xT[:, 0, 64:328], in_=fcw[:, :]), reads=[cbuf], writes=[xTb[0][0]])
                S.emit("dve", OP('tensor_copy', out=xT[:, 0, 328:352], in_=cwt[:, :]), reads=[cbuf], writes=[xTb[0][0]])
                S.emit("dve", OP('tensor_copy', out=xT[:, 0, 352:360], in_=lamt[:, :]), reads=[cbuf], writes=[xTb[0][0]])
                S.emit("dve", OP('tensor_copy', out=xT[:, 0, 384:512], in_=gsub[:, :]), reads=[cbuf], writes=[xTb[0][0]])
            if 'dbg_prenorm' in parts:
                S.barrier()
                AR.reset()
                ntd = NormTmp()
                prenorm(ntd, L * 4 + 0, range(NB))
                for c in range(KC):
                    for tb in range(NB):
                        S.emit("dve", OP('tensor_copy', out=xT[:, c, 512 * tb:512 * (tb + 1)],
                                                                         in_=hT[:, c, 512 * tb:512 * (tb + 1)]),
                               reads=[hTb[c][tb]], writes=[xTb[c][tb]])
            if 'mix' in parts:
                S.barrier()
                if L == 0:
                    phase_attn(L)
                else:
                    phase_conv(L)
            if 'ffn' in parts:
                S.barrier()
                phase_ffn(L)
        S.barrier()
        phase_store_y()
        S.wait_all("sp", out_toks)

        sems = {}
        for k in list(S.cnt.keys()):
            nm = "s_" + "_".join(str(t) for t in (k if isinstance(k, tuple) else (k,)))
            sems[k] = E(nc.semaphore(nm))
        block = E(nc.Block())

        def replay(eng_name):
            def run(e):
                for (waits, fns, tok, inc) in S.ops[eng_name]:
                    for (k, v) in waits:
                        e.wait_ge(sems[k], v)
                    ins = None
                    for fn in fns:
                        ins = fn(e)
                    if tok is not None and ins is not None:
                        ins.then_inc(sems[tok[0]], inc)
            return run

        block.tensor(replay("pe"))
        block.scalar(replay("act"))
        block.vector(replay("dve"))
        block.gpsimd(replay("pool"))
        block.sync(replay("sp"))
    return nc


def make_consts():
    c = {}
    c["c_ident"] = np.eye(128, dtype=np.float32)
    c["c_ones"] = np.ones((128, 128), dtype=np.float32)
    p = np.arange(128, dtype=np.float32)
    ka = np.zeros((128, 128), np.float32)
    ka[0] = p
    ka[1] = 1.0
    ka[2] = 1.0
    c["c_kaug"] = ka
    qr = np.arange(512)
    qa = np.zeros((128, 512), np.float32)
    qa[0] = 1.0
    qa[1] = -(qr % 128)
    qa[2] = -128.0 * (qr // 128)
    c["c_qaug"] = qa
    k = np.arange(128)[:, None]
    q = np.arange(128)[None, :]
    allowed = (k // 64) <= (q // 64)
    c["c_maskd"] = np.where(allowed, -np.abs(q - k).astype(np.float32), np.float32(NEG_BIG)).astype(np.float32)
    bt = np.zeros((128, 128), np.float32)
    for h in range(H):
        slope = 2.0 ** (-(h + 1))
        for j in range(16):
            bt[:, h * 16 + j] = -128.0 * j * slope
    c["c_btab"] = bt
    return c


_NC_CACHE = {}


def kernel(x, norm_g, attn_w_qkv, attn_w_o, attn_lambda_q1, attn_lambda_k1, attn_lambda_q2,
           attn_lambda_k2, attn_subln_g, conv_w_in, conv_w, conv_w_out, ffn_w_up, ffn_conv_w, ffn_w_down):
    n = 8
    f = lambda a: np.ascontiguousarray(np.asarray(a, dtype=np.float32))
    shared = {
        "norm_g": f(norm_g), "attn_w_qkv": f(attn_w_qkv), "attn_w_o": f(attn_w_o),
        "attn_lambda_q1": f(attn_lambda_q1), "attn_lambda_k1": f(attn_lambda_k1),
        "attn_lambda_q2": f(attn_lambda_q2), "attn_lambda_k2": f(attn_lambda_k2),
        "attn_subln_g": f(attn_subln_g), "conv_w_in": f(conv_w_in), "conv_w": f(conv_w),
        "conv_w_out": f(conv_w_out), "ffn_w_up": f(ffn_w_up), "ffn_conv_w": f(ffn_conv_w),
        "ffn_w_down": f(ffn_w_down),
    }
    shared.update(make_consts())
    xs = f(x)
    nc = build_program((0, 1))
    in_maps = []
    for i in range(n):
        m = dict(shared)
        m["x"] = np.ascontiguousarray(xs[i])
        in_maps.append(m)
    res = run_bass_kernel_spmd(nc, in_maps, core_ids=list(range(n)))
    return np.stack([np.asarray(r["y"], dtype=np.float32) for r in res.results], axis=0)
```

```python
import math
from contextlib import ExitStack

import numpy as np
import concourse.bass as bass
import concourse.mybir as mybir
from concourse.bass_utils import run_bass_kernel_spmd

F32 = mybir.dt.float32
BF16 = mybir.dt.bfloat16
AF = mybir.ActivationFunctionType
ALU = mybir.AluOpType

S_LEN = 2048
D = 1024
NB = 4
KC = 8
DFF = 2816
NFI = 22
H = 8
NORM_EPS = 1e-6
SUBLN_EPS = 1e-5
NEG_BIG = -60000.0

ENGS = ("pe", "act", "dve", "pool", "sp")


def OP(name, *args, **kw):
    return lambda e: getattr(e, name)(*args, **kw)


class Buf:
    __slots__ = ("name", "w", "r")

    def __init__(self, name=""):
        self.name = name
        self.w = None
        self.r = {}


class Sched:
    def __init__(self):
        self.ops = {e: [] for e in ENGS}
        self.cnt = {}
        self.seen = {e: {} for e in ENGS}
        self.dma_keys = []

    def _waits(self, eng, reads, writes):
        deps = {}

        def add(tok):
            if tok is None:
                return
            k, v = tok
            if deps.get(k, 0) < v:
                deps[k] = v

        for b in reads:
            add(b.w)
        for b in writes:
            add(b.w)
            for k, v in b.r.items():
                add((k, v))
        waits = []
        for k, v in deps.items():
            if eng == "pe" and k == "pe":
                continue
            if self.seen[eng].get(k, 0) >= v:
                continue
            self.seen[eng][k] = v
            waits.append((k, v))
        return waits

    def _commit(self, tok, reads, writes):
        for b in reads:
            if b.r.get(tok[0], 0) < tok[1]:
                b.r[tok[0]] = tok[1]
        for b in writes:
            b.w = tok
            b.r = {}

    def emit(self, eng, fns, reads=(), writes=()):
        if not isinstance(fns, (list, tuple)):
            fns = [fns]
        waits = self._waits(eng, reads, writes)
        self.cnt[eng] = self.cnt.get(eng, 0) + 1
        tok = (eng, self.cnt[eng])
        self.ops[eng].append((waits, list(fns), tok, 1))
        self._commit(tok, reads, writes)
        return tok

    def dma(self, qeng, fn, key, reads=(), writes=(), nowait=False):
        waits = [] if nowait else self._waits(qeng, reads, writes)
        if key not in self.cnt:
            self.cnt[key] = 0
            self.dma_keys.append(key)
        self.cnt[key] += 16
        tok = (key, self.cnt[key])
        self.ops[qeng].append((waits, [fn], tok, 16))
        self._commit(tok, reads, writes)
        return tok

    def barrier(self):
        for e in ENGS:
            waits = []
            for k in ("pe", "act", "dve", "pool"):
                v = self.cnt.get(k, 0)
                if v == 0 or (e == k and e == "pe"):
                    continue
                if self.seen[e].get(k, 0) >= v:
                    continue
                self.seen[e][k] = v
                waits.append((k, v))
            if waits:
                self.ops[e].append((waits, [], None, 0))

    def wait_all(self, eng, toks):
        waits = []
        for k, v in toks:
            if self.seen[eng].get(k, 0) >= v:
                continue
            self.seen[eng][k] = v
            waits.append((k, v))
        if waits:
            self.ops[eng].append((waits, [], None, 0))


def build_program(layers=(0, 1), parts=('mix', 'ffn')):
    nc = bass.Bass("TRN2", target_bir_lowering=False)
    S = Sched()

    def din(name, shape):
        return nc.dram_tensor(name, list(shape), F32, kind="ExternalInput").ap()

    x_d = din("x", (S_LEN, D))
    norm_g_d = din("norm_g", (2, 4, D))
    wqkv_d = din("attn_w_qkv", (1, D, 3072))
    wo_d = din("attn_w_o", (1, D, D))
    lq1_d = din("attn_lambda_q1", (1, 64))
    lk1_d = din("attn_lambda_k1", (1, 64))
    lq2_d = din("attn_lambda_q2", (1, 64))
    lk2_d = din("attn_lambda_k2", (1, 64))
    subg_d = din("attn_subln_g", (1, 128))
    win_d = din("conv_w_in", (1, D, 3072))
    cw_d = din("conv_w", (1, 3, D))
    wout_d = din("conv_w_out", (1, D, D))
    wup_d = din("ffn_w_up", (2, D, 2 * DFF))
    fcw_d = din("ffn_conv_w", (2, 3, 2 * DFF))
    wdn_d = din("ffn_w_down", (2, DFF, D))
    ident_d = din("c_ident", (128, 128))
    ones_d = din("c_ones", (128, 128))
    augq_d = din("c_augq", (64, S_LEN))
    augk_d = din("c_augk", (64, S_LEN))
    maskd_d = din("c_maskd", (128, 128))
    y_d = nc.dram_tensor("y", [S_LEN, D], F32, kind="ExternalOutput").ap()

    es = ExitStack()
    with es:
        E = es.enter_context
        xT = E(nc.sbuf_tensor("xT", [128, KC, S_LEN], F32))
        hT = E(nc.sbuf_tensor("hT", [128, KC, S_LEN], BF16))
        NSLOT = 3
        wring = E(nc.sbuf_tensor("wring", [128, NSLOT, 4096], BF16))
        identf = E(nc.sbuf_tensor("identf", [128, 128], F32))
        identb = E(nc.sbuf_tensor("identb", [128, 128], BF16))
        onesb = E(nc.sbuf_tensor("onesb", [128, 128], BF16))
        maskd = E(nc.sbuf_tensor("maskd", [128, 128], BF16))
        mhalf = E(nc.sbuf_tensor("mhalf", [128, 2], F32))
        epst = E(nc.sbuf_tensor("epst", [128, 2], F32))
        g32 = E(nc.sbuf_tensor("g32", [128, 64], F32))
        fcw = E(nc.sbuf_tensor("fcw", [128, 264], F32))
        cwt = E(nc.sbuf_tensor("cwt", [128, 24], F32))
        gsub = E(nc.sbuf_tensor("gsub", [128, 128], F32))
        lamt = E(nc.sbuf_tensor("lamt", [128, 8], F32))
        pstage = E(nc.sbuf_tensor("pstage", [128, 128], F32))
        lstage = E(nc.sbuf_tensor("lstage", [128, 4, 64], F32))
        ARENA_W = 20700
        arena = E(nc.sbuf_tensor("arena", [128, ARENA_W], F32))
        ps = E(nc.psum_tensor("ps", [128, 8, 512], F32))
        pb = [Buf("pb%d" % i) for i in range(8)]

        class Arena:
            def __init__(self):
                self.off = 0

            def reset(self):
                self.off = 0

            def f32(self, n):
                a = arena[:, self.off:self.off + n]
                self.off += n
                assert self.off <= ARENA_W, self.off
                return a

            def bf16(self, n):
                w = (n + 1) // 2
                a = arena[:, self.off:self.off + w].bitcast(BF16)
                self.off += w
                assert self.off <= ARENA_W, self.off
                return a

        AR = Arena()

        xTb = [[Buf("xT%d_%d" % (c, tb)) for tb in range(NB)] for c in range(KC)]
        hTb = [[Buf("hT%d_%d" % (c, tb)) for tb in range(NB)] for c in range(KC)]
        cbuf = Buf("consts")

        bank_rr = [0]

        def next_bank(lo=0, n=4):
            b = lo + bank_rr[0] % n
            bank_rr[0] += 1
            return b

        wslot_buf = [Buf("wslot%d" % i) for i in range(NSLOT)]
        wplan = []
        wstate = {"issued": 0, "taken": 0}

        def w_issue_upto(n):
            while wstate["issued"] < min(n, len(wplan)):
                i = wstate["issued"]
                slot = i % NSLOT
                for pi, (dst_fn, src) in enumerate(wplan[i]):
                    dst = dst_fn(slot)
                    S.dma("pool", OP('dma_start', out=dst, in_=src),
                          ("w", slot), writes=[wslot_buf[slot]], nowait=(pi > 0))
                wstate["issued"] += 1

        def w_next():
            i = wstate["taken"]
            w_issue_upto(i + 1)
            wstate["taken"] += 1
            return i % NSLOT, wslot_buf[i % NSLOT]

        def w_prefetch():
            w_issue_upto(wstate["taken"] + NSLOT)

        def wview(slot, dims):
            n = int(np.prod(dims))
            a = wring[:, slot, 0:n]
            if len(dims) == 2:
                return a.rearrange("p (a b) -> p a b", a=dims[0])
            if len(dims) == 3:
                return a.rearrange("p (a b c) -> p a b c", a=dims[0], b=dims[1])
            return a

        def plan_weights():
            def rows(wd):
                return wd.rearrange("(kc p) n -> p kc n", p=128)

            for L in layers:
                if 'mix' not in parts:
                    pass
                elif L == 0:
                    w = rows(wqkv_d[0])
                    for g in range(2):
                        wplan.append([(lambda s: wview(s, (KC, 512)), w[:, :, 2048 + 512 * g:2048 + 512 * (g + 1)])])
                    for h in range(H):
                        ld = []
                        for j, base in enumerate((0, 512, 1024, 1536)):
                            ld.append((lambda s, j=j: wview(s, (KC, 256))[:, :, 64 * j:64 * (j + 1)],
                                       w[:, :, base + 64 * h:base + 64 * (h + 1)]))
                        wplan.append(ld)
                    w = rows(wo_d[0])
                    for g in range(2):
                        wplan.append([(lambda s: wview(s, (KC, 512)), w[:, :, 512 * g:512 * (g + 1)])])
                else:
                    w = rows(win_d[0])
                    for j in range(KC):
                        ld = []
                        for sec in range(3):
                            ld.append((lambda s, sec=sec: wview(s, (KC, 3, 128))[:, :, sec, :],
                                       w[:, :, sec * 1024 + 128 * j:sec * 1024 + 128 * (j + 1)]))
                        wplan.append(ld)
                    w = rows(wout_d[0])
                    for g in range(2):
                        wplan.append([(lambda s: wview(s, (KC, 512)), w[:, :, 512 * g:512 * (g + 1)])])
                wu = rows(wup_d[L])
                wd = wdn_d[L].rearrange("(fi p) n -> p fi n", p=128)
                for tb in range(NB if 'ffn' in parts else 0):
                    for fp in range(NFI // 2):
                        ld = []
                        for gv in range(2):
                            ld.append((lambda s, gv=gv: wview(s, (2, KC, 256))[:, gv, :, :],
                                       wu[:, :, gv * DFF + 256 * fp:gv * DFF + 256 * (fp + 1)]))
                        wplan.append(ld)
                    for c in range(KC):
                        wplan.append([(lambda s: wview(s, (NFI, 128)), wd[:, :, 128 * c:128 * (c + 1)])])

        plan_weights()

        def setup():
            def cload(q, dst, src, key):
                S.dma(q, OP('dma_start', out=dst, in_=src), key, writes=[cbuf])

            cload("sp", identf[:, :], ident_d[:, :], "c0")
            cload("pool", identb[:, :], ident_d[:, :], "c1")
            cload("pool", onesb[:, :], ones_d[:, :], "c1")
            cload("pool", maskd[:, :], maskd_d[:, :], "c1")
            S.emit("pool", OP('memset', mhalf[:, :], -0.5), writes=[cbuf])
            S.emit("pool", OP('memset', epst[:, :], float(D * NORM_EPS)), writes=[cbuf])
            pst = Buf("pstage")

            def tr_param(src2d, nrows, dst, scale):
                S.dma("sp", (OP('dma_start', out=pstage[0:nrows, :], in_=src2d)), "c2", writes=[pst])
                bank = next_bank()
                S.emit("pe", OP('transpose', ps[:, bank, 0:nrows], pstage[0:nrows, :], identf[0:nrows, 0:nrows]),
                       reads=[pst, cbuf], writes=[pb[bank]])
                S.emit("dve", OP('tensor_scalar', out=dst, in0=ps[:, bank, 0:nrows], scalar1=float(scale),
                                                       scalar2=None, op0=ALU.mult),
                       reads=[pb[bank]], writes=[cbuf])

            tr_param(norm_g_d.rearrange("l i (c p) -> (l i c) p", p=128), 64, g32[:, :], 32.0)
            f2 = fcw_d.rearrange("l j (c p) -> (l j c) p", p=128)
            for i in range(3):
                tr_param(f2[88 * i:88 * (i + 1), :], 88, fcw[:, 88 * i:88 * (i + 1)], 1.0)
            tr_param(cw_d.rearrange("l j (c p) -> (l j c) p", p=128), 24, cwt[:, :], 1.0)
            lam_init = 0.8 - 0.6 * math.exp(-0.3 * 0)
            S.dma("sp", (OP('dma_start', out=gsub[:, :], in_=subg_d[0:1, :].broadcast_to([128, 128]))),
                  "c3", writes=[cbuf])
            S.emit("dve", OP('tensor_scalar', out=gsub[:, :], in0=gsub[:, :],
                                                   scalar1=float((1.0 - lam_init) * math.sqrt(128.0)),
                                                   scalar2=None, op0=ALU.mult), reads=[cbuf], writes=[cbuf])
            for i, ldd in enumerate((lq1_d, lk1_d, lq2_d, lk2_d)):
                S.dma("sp", (OP('dma_start', out=lstage[:, i, :],
                                                                 in_=ldd[0:1, :].broadcast_to([128, 64]))),
                      "c3", writes=[cbuf])
            S.emit("dve", OP('tensor_tensor', out=lstage[:, 0, :], in0=lstage[:, 0, :], in1=lstage[:, 1, :],
                                                   op=ALU.mult), reads=[cbuf], writes=[cbuf])
            S.emit("dve", OP('tensor_tensor', out=lstage[:, 2, :], in0=lstage[:, 2, :], in1=lstage[:, 3, :],
                                                   op=ALU.mult), reads=[cbuf], writes=[cbuf])
            S.emit("dve", OP('reduce_sum', out=lamt[:, 0:1], in_=lstage[:, 0, :], axis=mybir.AxisListType.X),
                   reads=[cbuf], writes=[cbuf])
            S.emit("dve", OP('reduce_sum', out=lamt[:, 1:2], in_=lstage[:, 2, :], axis=mybir.AxisListType.X),
                   reads=[cbuf], writes=[cbuf])
            S.emit("act", OP('activation', out=lamt[:, 2:4], in_=lamt[:, 0:2], func=AF.Exp),
                   reads=[cbuf], writes=[cbuf])
            S.emit("dve", OP('tensor_tensor', out=lamt[:, 4:5], in0=lamt[:, 2:3], in1=lamt[:, 3:4],
                                                   op=ALU.subtract), reads=[cbuf], writes=[cbuf])
            S.emit("dve", OP('tensor_scalar', out=lamt[:, 5:6], in0=lamt[:, 4:5], scalar1=float(lam_init),
                                                   scalar2=None, op0=ALU.add), reads=[cbuf], writes=[cbuf])

        def phase_load_x():
            AR.reset()
            xs = AR.f32(8 * 1024).rearrange("p (s n) -> p s n", s=8)
            xsb = [Buf("xs%d" % i) for i in range(8)]

            def load(t):
                sl = t % 8
                S.dma("sp", (OP('dma_start', out=xs[:, sl, :], in_=x_d[128 * t:128 * (t + 1), :])),
                      ("xs", sl), writes=[xsb[sl]])

            for t in range(8):
                load(t)
            for tb in range(NB):
                for c in range(KC):
                    bank = next_bank()
                    fns = []
                    for j in range(4):
                        sl = (4 * tb + j) % 8
                        fns.append(OP('transpose',
                            ps[:, bank, 128 * j:128 * (j + 1)], xs[:, sl, 128 * c:128 * (c + 1)], identf[:, :]))
                    S.emit("pe", fns, reads=[xsb[(4 * tb + j) % 8] for j in range(4)] + [cbuf], writes=[pb[bank]])
                    dst = xT[:, c, 512 * tb:512 * (tb + 1)]
                    if c % 2 == 0:
                        S.emit("act", OP('activation', out=dst, in_=ps[:, bank, :], func=AF.Copy),
                               reads=[pb[bank]], writes=[xTb[c][tb]])
                    else:
                        S.emit("dve", OP('tensor_copy', out=dst, in_=ps[:, bank, :]),
                               reads=[pb[bank]], writes=[xTb[c][tb]])
                if tb + 2 < NB:
                    for j in range(4):
                        load(4 * (tb + 2) + j)

        out_toks = []

        def phase_store_y():
            AR.reset()
            ys = AR.f32(4 * 1024).rearrange("p (s n) -> p s n", s=4)
            ysb = [Buf("ys%d" % i) for i in range(4)]
            for t in range(16):
                tb = t // 4
                sl = t % 4
                for half in range(2):
                    bank = next_bank()
                    fns = []
                    for j in range(4):
                        c = 4 * half + j
                        fns.append(OP('transpose',
                            ps[:, bank, 128 * j:128 * (j + 1)], xT[:, c, 128 * t:128 * (t + 1)], identf[:, :]))
                    S.emit("pe", fns, reads=[xTb[4 * half + j][tb] for j in range(4)] + [cbuf], writes=[pb[bank]])
                    dst = ys[:, sl, 512 * half:512 * (half + 1)]
                    if half == 0:
                        S.emit("act", OP('activation', out=dst, in_=ps[:, bank, :], func=AF.Copy),
                               reads=[pb[bank]], writes=[ysb[sl]])
                    else:
                        S.emit("dve", OP('tensor_copy', out=dst, in_=ps[:, bank, :]),
                               reads=[pb[bank]], writes=[ysb[sl]])
                tok = S.dma("sp", (OP('dma_start', out=y_d[128 * t:128 * (t + 1), :], in_=ys[:, sl, :])),
                            ("ys", sl), reads=[ysb[sl]])
                out_toks.append(tok)

        class NormTmp:
            def __init__(self, t1n=2):
                self.sq = AR.bf16(4 * 512).rearrange("p (s n) -> p s n", s=4)
                self.sqb = [Buf("sq%d" % i) for i in range(4)]
                self.t1n = t1n
                self.t1 = AR.f32(t1n * 512).rearrange("p (s n) -> p s n", s=t1n)
                self.t1b = [Buf("t1_%d" % i) for i in range(t1n)]
                self.rstd = AR.f32(2 * 512).rearrange("p (s n) -> p s n", s=2)
                self.rstdb = [Buf("rstd%d" % i) for i in range(2)]
                self.k = 0
                self.r = 0

            def next_sq(self):
                i = self.k % 4
                self.k += 1
                return i

            def next_r(self):
                i = self.r % 2
                self.r += 1
                return i

            def t1i(self, r):
                return r % self.t1n

        def emit_rstd(nt, ssbank, eps_total):
            r = nt.next_r()
            ti = nt.t1i(r)
            S.emit("act", OP('activation', out=nt.t1[:, ti, :], in_=ps[:, ssbank, :], func=AF.Ln, bias=epst[:, 0:1]),
                   reads=[pb[ssbank], cbuf], writes=[nt.t1b[ti]])
            S.emit("act", OP('activation', out=nt.rstd[:, r, :], in_=nt.t1[:, ti, :], func=AF.Exp, scale=-0.5),
                   reads=[nt.t1b[ti]], writes=[nt.rstdb[r]])
            return r

        def prenorm(nt, gidx, tbs):
            for tb in tbs:
                blk = slice(512 * tb, 512 * (tb + 1))
                ssbank = 7
                for c in range(KC):
                    i = nt.next_sq()
                    S.emit("act", OP('activation', out=nt.sq[:, i, :], in_=xT[:, c, blk], func=AF.Square),
                           reads=[xTb[c][tb]], writes=[nt.sqb[i]])
                    S.emit("pe", OP('matmul', ps[:, ssbank, :], onesb[:, :], nt.sq[:, i, :],
                                                             start=(c == 0), stop=(c == KC - 1)),
                           reads=[nt.sqb[i], cbuf], writes=[pb[ssbank]])
                r = emit_rstd(nt, ssbank, D * NORM_EPS)
                for c in range(KC):
                    S.emit("dve", OP('scalar_tensor_tensor',
                        out=hT[:, c, blk], in0=xT[:, c, blk], scalar=g32[:, gidx * 8 + c:gidx * 8 + c + 1],
                        in1=nt.rstd[:, r, :], op0=ALU.mult, op1=ALU.mult),
                           reads=[xTb[c][tb], nt.rstdb[r], cbuf], writes=[hTb[c][tb]])

        class PostTmp:
            def __init__(self):
                self.msb = AR.f32(KC * 512).rearrange("p (c n) -> p c n", c=KC)
                self.msbb = [Buf("msb%d" % i) for i in range(KC)]
                self.tp = AR.f32(2 * 512).rearrange("p (s n) -> p s n", s=2)
                self.tpb = [Buf("tp%d" % i) for i in range(2)]
                self.k = 0

        def post_block(nt, pt, gidx, tb, mm_group, banks=(0, 4)):
            blk = slice(512 * tb, 512 * (tb + 1))
            ssbank = 7
            pending = None

            def ss_mm(c, i):
                S.emit("pe", OP('matmul', ps[:, ssbank, :], onesb[:, :], nt.sq[:, i, :],
                                                start=(c == 0), stop=(c == KC - 1)),
                       reads=[nt.sqb[i], cbuf], writes=[pb[ssbank]])

            for c in range(KC):
                bank = next_bank(*banks)
                mm_group(c, bank)
                if pending is not None:
                    ss_mm(*pending)
                S.emit("act", OP('activation', out=pt.msb[:, c, :], in_=ps[:, bank, :], func=AF.Copy),
                       reads=[pb[bank]], writes=[pt.msbb[c]])
                i = nt.next_sq()
                S.emit("act", OP('activation', out=nt.sq[:, i, :], in_=pt.msb[:, c, :], func=AF.Square),
                       reads=[pt.msbb[c]], writes=[nt.sqb[i]])
                pending = (c, i)
            ss_mm(*pending)
            r = emit_rstd(nt, ssbank, D * NORM_EPS)
            for c in range(KC):
                j = pt.k % 2
                pt.k += 1
                S.emit("dve", OP('scalar_tensor_tensor',
                    out=pt.tp[:, j, :], in0=pt.msb[:, c, :], scalar=g32[:, gidx * 8 + c:gidx * 8 + c + 1],
                    in1=nt.rstd[:, r, :], op0=ALU.mult, op1=ALU.mult),
                       reads=[pt.msbb[c], nt.rstdb[r], cbuf], writes=[pt.tpb[j]])
                S.emit("pool", OP('tensor_tensor', out=xT[:, c, blk], in0=xT[:, c, blk], in1=pt.tp[:, j, :],
                                                                  op=ALU.add),
                       reads=[pt.tpb[j], xTb[c][tb]], writes=[xTb[c][tb]])

        def proj_post(nt, pt, gidx, in_ap, in_bufs):
            s0, b0 = w_next()
            s1, b1 = w_next()
            wt = [wview(s0, (KC, 512)), wview(s1, (KC, 512))]
            wb = [b0, b1]
            for tb in range(NB):
                def grp(c, bank, tb=tb):
                    w = wt[c // 4]
                    col = 128 * (c % 4)
                    fns = [(OP('matmul', ps[:, bank, :], w[:, kc, col:col + 128], in_ap(kc, tb),
                                                      start=(kc == 0), stop=(kc == KC - 1))) for kc in range(KC)]
                    S.emit("pe", fns, reads=[wb[c // 4]] + in_bufs(tb), writes=[pb[bank]])

                post_block(nt, pt, gidx, tb, grp)
            w_prefetch()

        def phase_ffn(L):
            AR.reset()
            nt = NormTmp(t1n=1)
            pt = PostTmp()
            aT = AR.bf16(NFI * 512).rearrange("p (f n) -> p f n", f=NFI)
            aTb = [Buf("aT%d" % i) for i in range(NFI)]
            NU = 6
            ug = AR.f32(NU * 516).rearrange("p (s n) -> p s n", s=NU)
            ugb = [Buf("ug%d" % i) for i in range(NU)]
            tcv = AR.f32(NU * 512).rearrange("p (s n) -> p s n", s=NU)
            tcb = [Buf("tc%d" % i) for i in range(NU)]
            NSG = 2
            sg = AR.bf16(NSG * 512).rearrange("p (s n) -> p s n", s=NSG)
            sgb = [Buf("sg%d" % i) for i in range(NSG)]
            halo = AR.f32(44 * 2).rearrange("p (c n) -> p c n", c=44)
            halob = [Buf("halo%d" % i) for i in range(44)]
            uk = [0]
            sk = [0]
            pendB = []
            pendC = []

            def stageB(item):
                fi, ugate, uval = item
                si = sk[0] % NSG
                sk[0] += 1
                S.emit("act", OP('activation', out=sg[:, si, :], in_=tcv[:, ugate, :], func=AF.Silu),
                       reads=[tcb[ugate]], writes=[sgb[si]])
                return (fi, si, uval)

            def stageC(item):
                fi, si, uval = item
                S.emit("dve", OP('tensor_tensor', out=aT[:, fi, :], in0=sg[:, si, :], in1=tcv[:, uval, :], op=ALU.mult),
                       reads=[sgb[si], tcb[uval]], writes=[aTb[fi]])

            prenorm(nt, L * 4 + 2, [0])
            for tb in range(NB):
                blk = slice(512 * tb, 512 * (tb + 1))
                for fp in range(NFI // 2):
                    slot, wbuf = w_next()
                    wt = wview(slot, (2, KC, 256))
                    for f2 in range(2):
                        fi = 2 * fp + f2
                        tcs = []
                        for gv in range(2):
                            ch = gv * NFI + fi
                            bank = next_bank(0, 7)
                            fns = [OP('matmul', ps[:, bank, :], wt[:, gv, kc, 128 * f2:128 * (f2 + 1)], hT[:, kc, blk],
                                      start=(kc == 0), stop=(kc == KC - 1)) for kc in range(KC)]
                            S.emit("pe", fns, reads=[wbuf] + [hTb[kc][tb] for kc in range(KC)], writes=[pb[bank]])
                            u = uk[0] % NU
                            uk[0] += 1
                            if tb == 0:
                                S.emit("pool", OP('memset', ug[:, u, 0:2], 0.0), writes=[ugb[u]])
                            else:
                                S.emit("pool", OP('tensor_copy', out=ug[:, u, 0:2], in_=halo[:, ch, :]),
                                       reads=[halob[ch]], writes=[ugb[u]])
                            S.emit("act", OP('activation', out=ug[:, u, 2:514], in_=ps[:, bank, :], func=AF.Copy),
                                   reads=[pb[bank]], writes=[ugb[u]])
                            wi = lambda j, ch=ch: fcw[:, (L * 3 + j) * 44 + ch:(L * 3 + j) * 44 + ch + 1]
                            S.emit("act", OP('activation', out=tcv[:, u, :], in_=ps[:, bank, :], func=AF.Copy, scale=wi(2)),
                                   reads=[pb[bank], cbuf], writes=[tcb[u]])
                            if tb < NB - 1:
                                S.emit("pool", OP('tensor_copy', out=halo[:, ch, :], in_=ug[:, u, 512:514]),
                                       reads=[ugb[u]], writes=[halob[ch]])
                            S.emit("dve", OP('scalar_tensor_tensor', out=tcv[:, u, :], in0=ug[:, u, 1:513], scalar=wi(1),
                                             in1=tcv[:, u, :], op0=ALU.mult, op1=ALU.add),
                                   reads=[ugb[u], tcb[u], cbuf], writes=[tcb[u]])
                            S.emit("dve", OP('scalar_tensor_tensor', out=tcv[:, u, :], in0=ug[:, u, 0:512], scalar=wi(0),
                                             in1=tcv[:, u, :], op0=ALU.mult, op1=ALU.add),
                                   reads=[ugb[u], tcb[u], cbuf], writes=[tcb[u]])
                            tcs.append(u)
                        pendB.append((fi, tcs[0], tcs[1]))
                        if len(pendB) > 1:
                            pendC.append(stageB(pendB.pop(0)))
                        if len(pendC) > 1:
                            stageC(pendC.pop(0))
                    w_prefetch()
                while pendB:
                    pendC.append(stageB(pendB.pop(0)))
                while pendC:
                    stageC(pendC.pop(0))
                if tb + 1 < NB:
                    prenorm(nt, L * 4 + 2, [tb + 1])

                def grp(c, bank):
                    slot, wbuf = w_next()
                    wd = wview(slot, (NFI, 128))
                    fns = [OP('matmul', ps[:, bank, :], wd[:, fi, :], aT[:, fi, :],
                              start=(fi == 0), stop=(fi == NFI - 1)) for fi in range(NFI)]
                    S.emit("pe", fns, reads=[wbuf] + aTb, writes=[pb[bank]])
                    w_prefetch()

                post_block(nt, pt, L * 4 + 3, tb, grp, banks=(0, 7))

        def phase_conv(L):
            AR.reset()
            nt = NormTmp()
            pt = PostTmp()
            gT = AR.bf16(KC * S_LEN).rearrange("p (c n) -> p c n", c=KC)
            gTb = [[Buf("gT%d_%d" % (c, tb)) for tb in range(NB)] for c in range(KC)]
            uc = AR.f32(2 * 512).rearrange("p (s n) -> p s n", s=2)
            ucb = [Buf("uc%d" % i) for i in range(2)]
            z = AR.f32(2 * 516).rearrange("p (s n) -> p s n", s=2)
            zb = [Buf("z%d" % i) for i in range(2)]
            tcv = AR.f32(2 * 512).rearrange("p (s n) -> p s n", s=2)
            tcb = [Buf("tcc%d" % i) for i in range(2)]
            prenorm(nt, L * 4 + 0, range(NB))
            k = 0
            for j in range(KC):
                slot, wbuf = w_next()
                wt = wview(slot, (KC, 3, 128))
                for tb in range(NB):
                    blk = slice(512 * tb, 512 * (tb + 1))
                    banks = []
                    for sec in range(3):
                        bank = next_bank(0, 6)
                        fns = [(OP('matmul',
                            ps[:, bank, :], wt[:, kc, sec, :], hT[:, kc, blk],
                            start=(kc == 0), stop=(kc == KC - 1))) for kc in range(KC)]
                        S.emit("pe", fns, reads=[wbuf] + [hTb[kc][tb] for kc in range(KC)], writes=[pb[bank]])
                        banks.append(bank)
                    bb, bc, bh = banks
                    r = k % 2
                    k += 1
                    S.emit("act", OP('activation', out=uc[:, r, :], in_=ps[:, bc, :], func=AF.Copy),
                           reads=[pb[bc]], writes=[ucb[r]])
                    if tb == 0:
                        S.emit("pool", OP('memset', z[:, r, 0:2], 0.0), writes=[zb[r]])
                    else:
                        S.emit("pool", OP('tensor_copy', out=z[:, r, 0:2], in_=z[:, 1 - r, 512:514]),
                               reads=[zb[1 - r]], writes=[zb[r]])
                    S.emit("dve", OP('tensor_tensor', out=z[:, r, 2:514], in0=uc[:, r, :], in1=ps[:, bh, :],
                                                                      op=ALU.mult),
                           reads=[ucb[r], pb[bh]], writes=[zb[r]])
                    wi = lambda jj, j=j: cwt[:, jj * 8 + j:jj * 8 + j + 1]
                    S.emit("dve", OP('tensor_scalar', out=tcv[:, r, :], in0=z[:, r, 2:514], scalar1=wi(2),
                                                                     scalar2=None, op0=ALU.mult),
                           reads=[zb[r], cbuf], writes=[tcb[r]])
                    S.emit("dve", OP('scalar_tensor_tensor',
                        out=tcv[:, r, :], in0=z[:, r, 1:513], scalar=wi(1), in1=tcv[:, r, :], op0=ALU.mult, op1=ALU.add),
                           reads=[zb[r], tcb[r], cbuf], writes=[tcb[r]])
                    S.emit("dve", OP('scalar_tensor_tensor',
                        out=tcv[:, r, :], in0=z[:, r, 0:512], scalar=wi(0), in1=tcv[:, r, :], op0=ALU.mult, op1=ALU.add),
                           reads=[zb[r], tcb[r], cbuf], writes=[tcb[r]])
                    S.emit("dve", OP('tensor_tensor',
                        out=gT[:, j, blk], in0=tcv[:, r, :], in1=ps[:, bb, :], op=ALU.mult),
                           reads=[tcb[r], pb[bb]], writes=[gTb[j][tb]])
                w_prefetch()
            proj_post(nt, pt, L * 4 + 1,
                      lambda kc, tb: gT[:, kc, 512 * tb:512 * (tb + 1)],
                      lambda tb: [gTb[kc][tb] for kc in range(KC)])

        def phase_attn(L):
            AR.reset()
            vflat = AR.bf16(H * 16 * 130)
            vaug = vflat.rearrange("p (h t e) -> p h t e", h=H, t=16)
            vb = [[Buf("v%d_%d" % (g, t)) for t in range(16)] for g in range(2)]
            oT0 = AR.bf16(S_LEN)

            def oTv(h):
                if h == 0:
                    return oT0
                return vflat[:, (h - 1) * 2080:(h - 1) * 2080 + S_LEN]

            oTb = [[Buf("oT%d_%d" % (h, tb)) for tb in range(NB)] for h in range(H)]
            Pt = AR.bf16(2 * 2 * 512).rearrange("p (s m n) -> p s m n", s=2, m=2)
            Ptb = [Buf("Pt%d" % i) for i in range(2)]
            NEP = 2
            rl = AR.f32(NEP * 4).rearrange("p (s n) -> p s n", s=NEP)
            t2 = AR.f32(NEP * 128).rearrange("p (s n) -> p s n", s=NEP)
            ot = AR.f32(NEP * 128).rearrange("p (s n) -> p s n", s=NEP)
            junk = t2
            sst = AR.f32(NEP * 4).rearrange("p (s n) -> p s n", s=NEP)
            epb = [Buf("ep%d" % i) for i in range(NEP)]
            NON = 4
            on = AR.bf16(NON * 128).rearrange("p (s n) -> p s n", s=NON)
            onb = [Buf("on%d" % i) for i in range(NON)]
            post_off = AR.off
            QA = AR.bf16(S_LEN)
            QB = AR.bf16(S_LEN)
            KA = AR.bf16(S_LEN)
            KB = AR.bf16(S_LEN)
            QQb = [Buf("QQ%d" % tb) for tb in range(NB)]
            KKb = [Buf("KK%d" % tb) for tb in range(NB)]
            augb = Buf("aug")
            nt0 = NormTmp()
            prenorm(nt0, L * 4 + 0, range(NB))
            S.barrier()
            for (dst, src) in ((QA[64:128, :], augq_d[:, :]), (QB[0:64, :], augq_d[:, :]),
                               (KA[64:128, :], augk_d[:, :]), (KB[0:64, :], augk_d[:, :])):
                S.dma("pool", OP('dma_start', out=dst, in_=src), "c4", writes=[augb])
            S.emit("pool", OP('memset', vaug[:, :, :, 128:129], 1.0), writes=[b for g in vb for b in g])
            ev = 0
            for g in range(2):
                slot, wbuf = w_next()
                wv = wview(slot, (KC, 512))
                for t in range(16):
                    bank = next_bank()
                    fns = [(OP('matmul',
                        ps[:, bank, :], hT[:, kc, 128 * t:128 * (t + 1)], wv[:, kc, :],
                        start=(kc == 0), stop=(kc == KC - 1))) for kc in range(KC)]
                    S.emit("pe", fns, reads=[wbuf] + [hTb[kc][t // 4] for kc in range(KC)], writes=[pb[bank]])
                    dst = vaug[:, 4 * g:4 * g + 4, t, 0:128]
                    src = ps[:, bank, :].rearrange("p (h e) -> p h e", h=4)
                    if ev % 2 == 0:
                        S.emit("act", OP('activation', out=dst, in_=src, func=AF.Copy),
                               reads=[pb[bank]], writes=[vb[g][t]])
                    else:
                        S.emit("dve", OP('tensor_copy', out=dst, in_=src),
                               reads=[pb[bank]], writes=[vb[g][t]])
                    ev += 1
                w_prefetch()
            pk = [0]
            epk = [0]
            onk = [0]
            deferred = []

            def flush_deferred():
                while deferred:
                    deferred.pop(0)()

            for h in range(H):
                slope = 2.0 ** (-(h + 1))
                qscale = 1.0 / (8.0 * slope)
                slot, wbuf = w_next()
                wqk = wview(slot, (KC, 256))
                for tb in range(NB):
                    blk = slice(512 * tb, 512 * (tb + 1))
                    for which in range(2):
                        bank = next_bank()
                        fns = [(OP('matmul',
                            ps[:, bank, :], wqk[:, kc, 128 * which:128 * (which + 1)], hT[:, kc, blk],
                            start=(kc == 0), stop=(kc == KC - 1))) for kc in range(KC)]
                        S.emit("pe", fns, reads=[wbuf] + [hTb[kc][tb] for kc in range(KC)], writes=[pb[bank]])
                        if which == 0:
                            S.emit("act", OP('activation', out=QA[0:64, blk], in_=ps[0:64, bank, :], func=AF.Copy,
                                             scale=float(qscale)), reads=[pb[bank]], writes=[QQb[tb]])
                            S.emit("dve", OP('tensor_scalar', out=QB[64:128, blk], in0=ps[64:128, bank, :],
                                             scalar1=float(qscale), scalar2=None, op0=ALU.mult),
                                   reads=[pb[bank]], writes=[QQb[tb]])
                        else:
                            S.emit("act", OP('activation', out=KA[0:64, blk], in_=ps[0:64, bank, :], func=AF.Copy),
                                   reads=[pb[bank]], writes=[KKb[tb]])
                            S.emit("dve", OP('tensor_copy', out=KB[64:128, blk], in_=ps[64:128, bank, :]),
                                   reads=[pb[bank]], writes=[KKb[tb]])
                w_prefetch()
                steps = [(qb, kt) for qb in range(NB) for kt in range(4 * qb + 4)]

                def step_geom(qb, kt):
                    lo = max(128 * kt, 512 * qb)
                    hi = 512 * (qb + 1)
                    return lo, hi, hi - lo, kt >= 4 * qb

                def emit_scores(i):
                    qb, kt = steps[i]
                    lo, hi, N, diag = step_geom(qb, kt)
                    par = (pk[0] + i) % 2
                    sb = 2 * par
                    fns = []
                    for m, (kt_, qt_) in enumerate(((KA, QA), (KB, QB))):
                        fns.append(OP('matmul', ps[:, sb + m, 0:N], kt_[:, 128 * kt:128 * (kt + 1)], qt_[:, lo:hi],
                                      start=True, stop=(not diag)))
                    if diag:
                        for m in range(2):
                            fns.append(OP('matmul', ps[:, sb + m, 0:128], identb[:, :], maskd[:, :], start=False, stop=True))
                    S.emit("pe", fns, reads=[KKb[kt // 4], QQb[qb], cbuf, augb], writes=[pb[sb], pb[sb + 1]])
                    S.emit("act", OP('activation', out=Pt[:, par, :, 0:N], in_=ps[:, sb:sb + 2, 0:N], func=AF.Exp,
                                     scale=float(slope)),
                           reads=[pb[sb], pb[sb + 1]], writes=[Ptb[par]])

                def emit_av(i):
                    qb, kt = steps[i]
                    lo, hi, N, diag = step_geom(qb, kt)
                    par = (pk[0] + i) % 2
                    fns = []
                    wr = []
                    for j in range(N // 128):
                        qt = lo // 128 + j
                        A = 4 + qt % 4
                        for m in range(2):
                            fns.append(OP('matmul', ps[:, A, 130 * m:130 * m + 129], Pt[:, par, m, 128 * j:128 * (j + 1)],
                                          vaug[:, h, kt, 0:129], start=(kt == 0 and m == 0), stop=(kt == qt),
                                          skip_group_check=True))
                        wr.append(pb[A])
                    S.emit("pe", fns, reads=[Ptb[par], vb[h // 4][kt]], writes=wr)
                    if kt >= 2:
                        flush_deferred()
                    if diag:
                        emit_epilogue(qb, kt)

                def emit_epilogue(qb, qt):
                    A = 4 + qt % 4
                    ep = epk[0] % NEP
                    epk[0] += 1
                    o_i = onk[0] % NON
                    onk[0] += 1
                    eb = epb[ep]
                    S.emit("dve", OP('reciprocal', out=rl[:, ep, 0:2], in_=ps[:, A, 128:259:130]),
                           reads=[pb[A]], writes=[eb])
                    S.emit("dve", OP('tensor_tensor', out=rl[:, ep, 2:3], in0=rl[:, ep, 1:2], in1=lamt[:, 5:6], op=ALU.mult),
                           reads=[eb, cbuf], writes=[eb])
                    S.emit("dve", OP('tensor_scalar', out=t2[:, ep, :], in0=ps[:, A, 130:258], scalar1=rl[:, ep, 2:3],
                                     scalar2=None, op0=ALU.mult), reads=[pb[A], eb], writes=[eb])
                    S.emit("dve", OP('scalar_tensor_tensor', out=ot[:, ep, :], in0=ps[:, A, 0:128], scalar=rl[:, ep, 0:1],
                                     in1=t2[:, ep, :], op0=ALU.mult, op1=ALU.subtract), reads=[pb[A], eb], writes=[eb])
                    S.emit("dve", OP('scalar_tensor_tensor', out=junk[:, ep, :], in0=ot[:, ep, :], scalar=1.0,
                                     in1=ot[:, ep, :], op0=ALU.mult, op1=ALU.mult, accum_out=sst[:, ep, 0:1]),
                           reads=[eb], writes=[eb])
                    S.emit("dve", OP('tensor_scalar', out=sst[:, ep, 1:2], in0=sst[:, ep, 0:1],
                                     scalar1=float(128.0 * SUBLN_EPS), scalar2=None, op0=ALU.add), reads=[eb], writes=[eb])
                    S.emit("pool", OP('tensor_tensor', out=sst[:, ep, 2:3], in0=sst[:, ep, 1:2], in1=mhalf[:, 0:1], op=ALU.pow),
                           reads=[eb, cbuf], writes=[eb])
                    S.emit("dve", OP('scalar_tensor_tensor', out=on[:, o_i, :], in0=ot[:, ep, :], scalar=sst[:, ep, 2:3],
                                     in1=gsub[:, :], op0=ALU.mult, op1=ALU.mult), reads=[eb, cbuf], writes=[onb[o_i]])

                    S.dma("sp", OP('dma_start_transpose', out=oTv(h)[:, 128 * qt:128 * (qt + 1)], in_=on[:, o_i, :]),
                          ("tr", o_i), reads=[onb[o_i]], writes=[oTb[h][qb]])

                emit_scores(0)
                for i in range(len(steps)):
                    if i + 1 < len(steps):
                        emit_scores(i + 1)
                    emit_av(i)
                pk[0] += len(steps)
                flush_deferred()
            flush_deferred()
            S.barrier()
            AR.off = post_off
            nt = NormTmp()
            pt = PostTmp()
            proj_post(nt, pt, L * 4 + 1,
                      lambda kc, tb: oTv(kc)[:, 512 * tb:512 * (tb + 1)],
                      lambda tb: [oTb[kc][tb] for kc in range(KC)])

        setup()
        w_prefetch()
        phase_load_x()
        for L in layers:
            if 'dbg_g32' in parts:
                S.emit("dve", OP('tensor_copy', out=xT[:, 0, 0:64], in_=g32[:, :]), reads=[cbuf], writes=[xTb[0][0]])
                S.emit("dve", OP('tensor_copy', out=xT[:, 0, 64:328], in_=fcw[:, :]), reads=[cbuf], writes=[xTb[0][0]])
                S.emit("dve", OP('tensor_copy', out=xT[:, 0, 328:352], in_=cwt[:, :]), reads=[cbuf], writes=[xTb[0][0]])
                S.emit("dve", OP('tensor_copy', out=xT[:, 0, 352:360], in_=lamt[:, :]), reads=[cbuf], writes=[xTb[0][0]])
                S.emit("dve", OP('tensor_copy', out=xT[:, 0, 384:512], in_=gsub[:, :]), reads=[cbuf], writes=[xTb[0][0]])
            if 'dbg_prenorm' in parts:
                S.barrier()
                AR.reset()
                ntd = NormTmp()
                prenorm(ntd, L * 4 + 0, range(NB))
                for c in range(KC):
                    for tb in range(NB):
                        S.emit("dve", OP('tensor_copy', out=xT[:, c, 512 * tb:512 * (tb + 1)],
                                                                         in_=hT[:, c, 512 * tb:512 * (tb + 1)]),
                               reads=[hTb[c][tb]], writes=[xTb[c][tb]])
            if 'mix' in parts:
                S.barrier()
                if L == 0:
                    phase_attn(L)
                else:
                    phase_conv(L)
            if 'ffn' in parts:
                S.barrier()
                phase_ffn(L)
        S.barrier()
        phase_store_y()
        S.wait_all("sp", out_toks)

        sems = {}
        for k in list(S.cnt.keys()):
            nm = "s_" + "_".join(str(t) for t in (k if isinstance(k, tuple) else (k,)))
            sems[k] = E(nc.semaphore(nm))
        block = E(nc.Block())

        def replay(eng_name):
            def run(e):
                for (waits, fns, tok, inc) in S.ops[eng_name]:
                    for (k, v) in waits:
                        e.wait_ge(sems[k], v)
                    ins = None
                    for fn in fns:
                        ins = fn(e)
                    if tok is not None and ins is not None:
                        ins.then_inc(sems[tok[0]], inc)
            return run

        block.tensor(replay("pe"))
        block.scalar(replay("act"))
        block.vector(replay("dve"))
        block.gpsimd(replay("pool"))
        block.sync(replay("sp"))
    return nc


def make_consts():
    c = {}
    c["c_ident"] = np.eye(128, dtype=np.float32)
    c["c_ones"] = np.ones((128, 128), dtype=np.float32)
    p = np.arange(128, dtype=np.float32)
    t = np.arange(S_LEN)
    aq = np.zeros((64, S_LEN), np.float32)
    aq[0] = 1.0
    aq[1] = 1.0
    aq[2] = -(t % 128)
    aq[3] = -128.0 * (t // 128)
    ak = np.zeros((64, S_LEN), np.float32)
    ak[0] = t % 128
    ak[1] = 128.0 * (t // 128)
    ak[2] = 1.0
    ak[3] = 1.0
    c["c_augq"] = aq
    c["c_augk"] = ak
    k = np.arange(128)[:, None]
    q = np.arange(128)[None, :]
    allowed = (k // 64) <= (q // 64)
    corr = np.where(k > q, -2.0 * (k - q), 0.0)
    c["c_maskd"] = np.where(allowed, corr, NEG_BIG).astype(np.float32)
    return c


_NC_CACHE = {}


def kernel(x, norm_g, attn_w_qkv, attn_w_o, attn_lambda_q1, attn_lambda_k1, attn_lambda_q2,
           attn_lambda_k2, attn_subln_g, conv_w_in, conv_w, conv_w_out, ffn_w_up, ffn_conv_w, ffn_w_down):
    n = 8
    f = lambda a: np.ascontiguousarray(np.asarray(a, dtype=np.float32))
    shared = {
        "norm_g": f(norm_g), "attn_w_qkv": f(attn_w_qkv), "attn_w_o": f(attn_w_o),
        "attn_lambda_q1": f(attn_lambda_q1), "attn_lambda_k1": f(attn_lambda_k1),
        "attn_lambda_q2": f(attn_lambda_q2), "attn_lambda_k2": f(attn_lambda_k2),
        "attn_subln_g": f(attn_subln_g), "conv_w_in": f(conv_w_in), "conv_w": f(conv_w),
        "conv_w_out": f(conv_w_out), "ffn_w_up": f(ffn_w_up), "ffn_conv_w": f(ffn_conv_w),
        "ffn_w_down": f(ffn_w_down),
    }
    shared.update(make_consts())
    xs = f(x)
    nc = build_program((0, 1))
    in_maps = []
    for i in range(n):
        m = dict(shared)
        m["x"] = np.ascontiguousarray(xs[i])
        in_maps.append(m)
    res = run_bass_kernel_spmd(nc, in_maps, core_ids=list(range(n)))
    return np.stack([np.asarray(r["y"], dtype=np.float32) for r in res.results], axis=0)
```
